# Optimizing a Trainium2 kernel written in Bass

```python
import math
import jax, jax.numpy as jnp
from jax import lax
import numpy as np

D_MODEL = 1024
BATCH = 4
SEQ = 8192
DEPTH = 2

N_A_LAYERS = DEPTH // 2
N_B_LAYERS = DEPTH - N_A_LAYERS
HEAD_DIM = 64
A_HEADS = D_MODEL // (2 * HEAD_DIM)
A_VDIM = 2 * HEAD_DIM
A_QK_WIDTH = A_HEADS * 2 * HEAD_DIM
A_V_WIDTH = A_HEADS * A_VDIM
B_HEADS = D_MODEL // HEAD_DIM
B_WIDTH = B_HEADS * HEAD_DIM
FFN_HIDDEN = 256 * ((8 * D_MODEL // 3 + 255) // 256)
BLOCK_Q = 128
NUM_BUCKETS = 32
MAX_DISTANCE = 128
N_MOD = 9
EPS = 1e-6

kernel_name = "yoco_diffattn_fox_macaron_adaln"


def rms_norm(x, g):
    xf = x.astype(jnp.float32)
    y = xf * lax.rsqrt(jnp.mean(xf * xf, axis=-1, keepdims=True) + EPS)
    return (y * g.astype(jnp.float32)).astype(x.dtype)


def modulate(h, shift, scale):
    return h * (1 + scale[:, None, :]) + shift[:, None, :]


def swiglu(h, w_in, w_out):
    g, u = jnp.split(h @ w_in, 2, axis=-1)
    return (jax.nn.silu(g) * u) @ w_out


def t5_bucket(rel):
    n = jnp.maximum(rel, 0)
    max_exact = NUM_BUCKETS // 2
    nf = jnp.maximum(n, 1).astype(jnp.float32)
    large = max_exact + (jnp.log(nf / max_exact) / math.log(MAX_DISTANCE / max_exact)
                         * (NUM_BUCKETS - max_exact)).astype(jnp.int32)
    large = jnp.minimum(large, NUM_BUCKETS - 1)
    return jnp.where(n < max_exact, n, large)


def to_blocks(t):
    b, s = t.shape[:2]
    t = t.reshape((b, s // BLOCK_Q, BLOCK_Q) + t.shape[2:])
    return jnp.moveaxis(t, 1, 0)


def from_blocks(t):
    t = jnp.moveaxis(t, 0, 1)
    return t.reshape((t.shape[0], t.shape[1] * t.shape[2]) + t.shape[3:])


def lambda_init_fn(layer):
    return 0.8 - 0.6 * math.exp(-0.3 * layer)


def diff_attention(h, w_qkv, w_o, lam_params, subln_g, rel_bias, lambda_init):
    b, s, _ = h.shape
    qkv = h @ w_qkv
    q, k, v = jnp.split(qkv, [A_QK_WIDTH, 2 * A_QK_WIDTH], axis=-1)
    q = q.reshape(b, s, A_HEADS, 2, HEAD_DIM)
    k = k.reshape(b, s, A_HEADS, 2, HEAD_DIM)
    v = v.reshape(b, s, A_HEADS, A_VDIM)
    lf = lam_params.astype(jnp.float32)
    lam = jnp.exp(jnp.sum(lf[0] * lf[1])) - jnp.exp(jnp.sum(lf[2] * lf[3])) + lambda_init
    table = rel_bias.astype(jnp.float32)
    k_pos = jnp.arange(s)
    scale = HEAD_DIM ** -0.5

    def block(args):
        qb, blk = args
        q_pos = blk * BLOCK_Q + jnp.arange(BLOCK_Q)
        rel = q_pos[:, None] - k_pos[None, :]
        bias = jnp.take(table, t5_bucket(rel), axis=0)
        bias = jnp.transpose(bias, (2, 0, 1))[None, :, None]
        logits = jnp.einsum('bqhmd,bkhmd->bhmqk', qb, k).astype(jnp.float32) * scale + bias
        logits = jnp.where(rel >= 0, logits, -jnp.inf)
        p = jax.nn.softmax(logits, axis=-1)
        w = p[:, :, 0] - lam * p[:, :, 1]
        return jnp.einsum('bhqk,bkhe->bqhe', w.astype(v.dtype), v)

    nb = s // BLOCK_Q
    o = from_blocks(lax.map(block, (to_blocks(q), jnp.arange(nb))))
    o = rms_norm(o, subln_g) * (1 - lambda_init)
    return o.reshape(b, s, A_V_WIDTH) @ w_o


def shared_kv(x, c_act, kv_ada_w, kv_ada_b, kv_norm_g, kv_w, fgate_w, fgate_b):
    b, s, _ = x.shape
    shift, scale = jnp.split(c_act @ kv_ada_w + kv_ada_b, 2, axis=-1)
    h = modulate(rms_norm(x, kv_norm_g), shift, scale)
    k, v = jnp.split(h @ kv_w, 2, axis=-1)
    k = k.reshape(b, s, B_HEADS, HEAD_DIM)
    v = v.reshape(b, s, B_HEADS, HEAD_DIM)
    log_f = jax.nn.log_sigmoid((h @ fgate_w + fgate_b).astype(jnp.float32))
    F = jnp.cumsum(log_f, axis=1)
    return k, v, F


def forgetting_attention(h, w_q, w_o, k, v, F):
    b, s, _ = h.shape
    q = (h @ w_q).reshape(b, s, B_HEADS, HEAD_DIM)
    k_pos = jnp.arange(s)
    F_k = jnp.transpose(F, (0, 2, 1))
    scale = HEAD_DIM ** -0.5

    def block(args):
        qb, Fqb, blk = args
        q_pos = blk * BLOCK_Q + jnp.arange(BLOCK_Q)
        causal = q_pos[:, None] >= k_pos[None, :]
        decay = jnp.transpose(Fqb, (0, 2, 1))[..., None] - F_k[:, :, None, :]
        logits = jnp.einsum('bqhd,bkhd->bhqk', qb, k).astype(jnp.float32) * scale + decay
        logits = jnp.where(causal, logits, -jnp.inf)
        p = jax.nn.softmax(logits, axis=-1)
        return jnp.einsum('bhqk,bkhd->bqhd', p.astype(v.dtype), v)

    nb = s // BLOCK_Q
    o = from_blocks(lax.map(block, (to_blocks(q), to_blocks(F), jnp.arange(nb))))
    return o.reshape(b, s, B_WIDTH) @ w_o


def setup_inputs(seed: int = 0) -> dict:
    key = jax.random.key(seed)
    ks = jax.random.split(key, 21)
    f32 = jnp.float32
    D = D_MODEL

    def nrm(k, shape, std):
        return jax.random.normal(k, shape, f32) * std

    return {
        "x": nrm(ks[0], (BATCH, SEQ, D), 1.0),
        "c": nrm(ks[1], (BATCH, D), 1.0),
        "ada_w": nrm(ks[2], (DEPTH, D, N_MOD * D), 0.5 * D ** -0.5),
        "ada_b": nrm(ks[3], (DEPTH, N_MOD * D), 0.02),
        "norm_g": 1.0 + nrm(ks[4], (DEPTH, 3, D), 0.02),
        "ffn_w_in": nrm(ks[5], (DEPTH, 2, D, 2 * FFN_HIDDEN), D ** -0.5),
        "ffn_w_out": nrm(ks[6], (DEPTH, 2, FFN_HIDDEN, D), FFN_HIDDEN ** -0.5),
        "a_w_qkv": nrm(ks[7], (N_A_LAYERS, D, 2 * A_QK_WIDTH + A_V_WIDTH), D ** -0.5),
        "a_w_o": nrm(ks[8], (N_A_LAYERS, A_V_WIDTH, D), A_V_WIDTH ** -0.5),
        "a_lambda": nrm(ks[9], (N_A_LAYERS, 4, HEAD_DIM), 0.1),
        "a_subln_g": 1.0 + nrm(ks[10], (N_A_LAYERS, A_VDIM), 0.02),
        "rel_bias": nrm(ks[11], (NUM_BUCKETS, A_HEADS), 0.5),
        "kv_ada_w": nrm(ks[12], (D, 2 * D), 0.5 * D ** -0.5),
        "kv_ada_b": nrm(ks[13], (2 * D,), 0.02),
        "kv_norm_g": 1.0 + nrm(ks[14], (D,), 0.02),
        "kv_w": nrm(ks[15], (D, 2 * B_WIDTH), D ** -0.5),
        "fgate_w": nrm(ks[16], (D, B_HEADS), D ** -0.5),
        "fgate_b": jax.random.uniform(ks[17], (B_HEADS,), f32, minval=1.0, maxval=4.0),
        "b_w_q": nrm(ks[18], (N_B_LAYERS, D, B_WIDTH), D ** -0.5),
        "b_w_o": nrm(ks[19], (N_B_LAYERS, B_WIDTH, D), B_WIDTH ** -0.5),
        "final_g": 1.0 + nrm(ks[20], (D,), 0.02),
    }


def reference(x, c, ada_w, ada_b, norm_g, ffn_w_in, ffn_w_out, a_w_qkv, a_w_o, a_lambda,
              a_subln_g, rel_bias, kv_ada_w, kv_ada_b, kv_norm_g, kv_w, fgate_w, fgate_b,
              b_w_q, b_w_o, final_g):
    c_act = jax.nn.silu(c)
    k_sh = v_sh = F_sh = None
    for layer in range(DEPTH):
        if layer == N_A_LAYERS:
            k_sh, v_sh, F_sh = shared_kv(x, c_act, kv_ada_w, kv_ada_b, kv_norm_g, kv_w,
                                         fgate_w, fgate_b)
        mod = c_act @ ada_w[layer] + ada_b[layer]
        sh1, sc1, g1, sh2, sc2, g2, sh3, sc3, g3 = jnp.split(mod, N_MOD, axis=-1)
        h = modulate(rms_norm(x, norm_g[layer, 0]), sh1, sc1)
        x = x + 0.5 * g1[:, None, :] * swiglu(h, ffn_w_in[layer, 0], ffn_w_out[layer, 0])
        h = modulate(rms_norm(x, norm_g[layer, 1]), sh2, sc2)
        if layer < N_A_LAYERS:
            mix = diff_attention(h, a_w_qkv[layer], a_w_o[layer], a_lambda[layer],
                                 a_subln_g[layer], rel_bias, lambda_init_fn(layer))
        else:
            j = layer - N_A_LAYERS
            mix = forgetting_attention(h, b_w_q[j], b_w_o[j], k_sh, v_sh, F_sh)
        x = x + g2[:, None, :] * mix
        h = modulate(rms_norm(x, norm_g[layer, 2]), sh3, sc3)
        x = x + 0.5 * g3[:, None, :] * swiglu(h, ffn_w_in[layer, 1], ffn_w_out[layer, 1])
    return rms_norm(x, final_g)
```

```python
import math
from contextlib import ExitStack

import numpy as np
import concourse.bass as bass
import concourse.mybir as mybir
from concourse.bass_utils import run_bass_kernel_spmd

F32 = mybir.dt.float32
BF16 = mybir.dt.bfloat16
AF = mybir.ActivationFunctionType
ALU = mybir.AluOpType

D = 1024
S = 8192
NB = 4
NR = 2
NT = S // NR
NCORE = 8
FH = 2816
EPS = 1e-6
NEG = -1.0e4
NBLK = NT // 128
NZ = NR * 4 + 1
SAME_SYNC = True
WSHARD = False


class KB:
    def __init__(self, nc, es):
        self.nc = nc
        self.E = {"pe": nc.tensor, "act": nc.scalar, "dve": nc.vector, "pool": nc.gpsimd, "sp": nc.sync}
        self.sem = {}
        for e in self.E:
            self.sem[e] = es.enter_context(nc.semaphore("s_" + e))
        self.sem["cc"] = es.enter_context(nc.semaphore("s_cc"))
        self.cnt = {k: 0 for k in self.sem}
        self.KD = 6
        self.dq = {}
        for q in ("sp", "pool"):
            for i in range(self.KD):
                k = ("d", q, i)
                self.sem[k] = es.enter_context(nc.semaphore("d_%s%d" % (q, i)))
                self.cnt[k] = 0
            self.dq[q] = 0
        self.waited = {e: {} for e in self.E}
        self.lastw = {}
        self.readers = {}
        self.nins = 0

    def _wait(self, eng, s, v):
        if v <= 0:
            return
        if s == eng and (eng == "pe" or not SAME_SYNC):
            return
        if self.waited[eng].get(s, 0) >= v:
            return
        self.E[eng].wait_ge(self.sem[s], v)
        self.waited[eng][s] = v

    def _deps(self, eng, reads, writes):
        for k in reads:
            t = self.lastw.get(k)
            if t is not None:
                self._wait(eng, t[0], t[1])
        for k in writes:
            t = self.lastw.get(k)
            if t is not None:
                self._wait(eng, t[0], t[1])
            for s, v in self.readers.get(k, {}).items():
                self._wait(eng, s, v)

    def _record(self, tok, reads, writes):
        for k in reads:
            d = self.readers.setdefault(k, {})
            if d.get(tok[0], 0) < tok[1]:
                d[tok[0]] = tok[1]
        for k in writes:
            self.lastw[k] = tok
            self.readers[k] = {}

    def op(self, eng, fn, reads=(), writes=(), sig=True):
        self._deps(eng, reads, writes)
        ins = fn(self.E[eng])
        self.nins += 1
        if sig:
            self.cnt[eng] += 1
            ins.then_inc(self.sem[eng], 1)
            tok = (eng, self.cnt[eng])
        else:
            tok = (eng, self.cnt[eng] + 1)
        self._record(tok, reads, writes)

    def dma(self, q, out, in_, reads=(), writes=()):
        i = self.dq[q]
        self.dq[q] = (i + 1) % self.KD
        k = ("d", q, i)
        self._wait(q, k, self.cnt[k])
        self._deps(q, reads, writes)
        self.E[q].dma_start(out=out, in_=in_).then_inc(self.sem[k], 16)
        self.nins += 1
        self.cnt[k] += 16
        self._record((k, self.cnt[k]), reads, writes)

    def collective_rows(self, loc, allt, rows_per):
        R = loc.shape[0]
        for c0 in range(0, R, rows_per):
            ci = c0 // rows_per
            self.collective(loc[c0:c0 + rows_per, :], allt[ci * NR * rows_per:(ci + 1) * NR * rows_per, :], [], [])

    def collective(self, ins_ap, outs_ap, reads, writes, allcores=False):
        self._deps("pool", reads, writes)
        if allcores:
            groups = [list(range(NCORE))]
        else:
            groups = [[b * NR + r for r in range(NR)] for b in range(NCORE // NR)]
        self.nc.gpsimd.collective_compute("AllGather", ALU.bypass, replica_groups=groups,
                                          ins=[ins_ap], outs=[outs_ap]).then_inc(self.sem["cc"], 1)
        self.cnt["cc"] += 1
        self._record(("cc", self.cnt["cc"]), reads, writes)

    def barrier(self, cc=False):
        for e in self.E:
            for s in self.sem:
                if s != e and (cc or s != "cc"):
                    self._wait(e, s, self.cnt[s])
        self.lastw = {k: v for k, v in self.lastw.items() if isinstance(k, tuple) and k[0] == "W"}
        self.readers = {}


def build(stop_after=None, debug=False):
    nc = bass.Bass("TRN2", target_bir_lowering=False)
    dbg_kind = "ExternalOutput" if debug else "Internal"

    def din(name, shape, dt=F32):
        return nc.dram_tensor(name, list(shape), dt, kind="ExternalInput").ap()

    dbg_copies = []
    coll_written = []

    def dscr(name, shape, dt=F32, out=False, coll=False):
        if coll:
            t = nc.dram_tensor(name + "_i", list(shape), dt).ap()
            if debug:
                dbg_copies.append((nc.dram_tensor(name, list(shape), dt, kind="ExternalOutput").ap(), t))
            return t
        if out or debug:
            return nc.dram_tensor(name, list(shape), dt, kind="ExternalOutput").ap()
        return nc.dram_tensor(name, list(shape), dt).ap()

    wgather = []

    def dweight(name, R, C):
        if not WSHARD:
            return din(name, [R, C]), None
        sh = din(name, [R // NCORE, C])
        shi = nc.dram_tensor(name + "_shi", [R // NCORE, C], F32).ap()
        full = nc.dram_tensor(name + "_full", [R, C], F32).ap()
        wgather.append((name, sh, shi, full))
        return full, ("W", name)

    xT_in = din("xT", [D, NT])
    cT = din("cT", [128, 8])
    ada_w2, ada_wk = dweight("ada_w", 2 * D, 9 * D)
    ada_w = ada_w2.rearrange("(l d) f -> l d f", l=2)
    ada_bT = din("ada_bT", [128, 144])
    norm_gT = din("norm_gT", [128, 48])
    kv_ada_w, kv_ada_wk = dweight("kv_ada_w", D, 2 * D)
    ffn_w_in = {}
    ffn_w_out = {}
    for l_ in range(2):
        for s_2 in range(2):
            if (l_, s_2) == (0, 1):
                a_w_qkv, a_w_qkvk = dweight("a_w_qkv", D, 3 * D)
                a_w_o, a_w_ok = dweight("a_w_o", D, D)
            if (l_, s_2) == (1, 0):
                kv_w, kv_wk = dweight("kv_w", D, 2 * D)
            if (l_, s_2) == (1, 1):
                b_w_q, b_w_qk = dweight("b_w_q", D, D)
                b_w_o, b_w_ok = dweight("b_w_o", D, D)
            ffn_w_in[l_, s_2] = dweight("ffn_w_in_%d%d" % (l_, s_2), D, 2 * FH)
            ffn_w_out[l_, s_2] = dweight("ffn_w_out_%d%d" % (l_, s_2), FH, D)
    a_lam = din("a_lam", [1, 256])
    a_subg = din("a_subg", [128, 1])
    relb = din("relb", [128, 256])
    kv_ada_bT = din("kv_ada_bT", [128, 16])
    kv_norm_gT = din("kv_norm_gT", [128, 8])
    fgate_wT = din("fgate_wT", [128, 128])
    fgate_bT = din("fgate_bT", [16, 1])
    final_gT = din("final_gT", [128, 8])
    onehot = din("onehot", [128, 2 * 32 * 128])
    negc = din("negc", [128, 128])
    coefA = din("coefA", [128, NZ * 4 * 3])
    maskB = din("maskB", [128, NZ * 512])
    rsel = din("rsel", [128, 2])
    ident_in = din("ident", [128, 128])

    outT = dscr("outT", [D, NT], out=True)
    x_s = dscr("x_s", [D, NT])
    qT = dscr("qT", [D, NT], BF16)
    kT_loc = dscr("kT_loc", [D, NT], BF16, coll=True)
    kT_all = dscr("kT_all", [NR * D, NT], BF16, coll=True)
    v_loc = dscr("v_loc", [D, NT], BF16, coll=True)
    v_all = dscr("v_all", [NR * D, NT], BF16, coll=True)
    attnT = dscr("attnT", [D, NT], BF16)
    vb_loc = dscr("vb_loc", [16 * 128, NBLK * 65], BF16, coll=True)
    vb_all = dscr("vb_all", [NR * 16 * 128, NBLK * 65], BF16, coll=True)
    lf_loc = dscr("lf_loc", [16, NT], coll=True)
    lf_all = dscr("lf_all", [NR * 16, NT], coll=True)
    FkT = dscr("FkT", [16 * 3, NR * NT], BF16)
    FqT = dscr("FqT", [16 * 3, NT], BF16)

    es = ExitStack()
    es.enter_context(nc.allow_low_precision("bf16 matmul operands, fp32 accumulation"))
    K = KB(nc, es)

    uid = [0]

    def sb(name, shape, dt, stack=es):
        uid[0] += 1
        return stack.enter_context(nc.sbuf_tensor("%s_%d" % (name, uid[0]), list(shape), dt))

    psbig = es.enter_context(nc.psum_tensor("psbig", [128, 4096], F32))
    ps = [psbig[:, i * 512:(i + 1) * 512] for i in range(8)]

    for (wname, sh, shi, full) in wgather:
        K.dma("sp", shi, sh, writes=[("Wshi", wname)])
    for (wname, sh, shi, full) in wgather:
        K.collective(shi, full, reads=[("Wshi", wname)], writes=[("W", wname)], allcores=True)

    ones_bf = sb("ones_bf", [128, 128], BF16)
    ones_f = sb("ones_f", [128, 128], F32)
    modA = sb("modA", [128, 8, 8], F32)
    modB = sb("modB", [128, 8, 8], F32)
    modG = sb("modG", [128, 6, 8], F32)
    cact = sb("cact", [128, 8], F32)
    modraw = sb("modraw", [128, 160], F32)
    bias_all = sb("bias_all", [128, 160], F32)
    ng_all = sb("ng_all", [128, 64], F32)
    lam_sb = sb("lam_sb", [1, 256], F32)
    lam_t = sb("lam_t", [1, 8], F32)
    nlam = sb("nlam", [128, 1], F32)
    gsub = sb("gsub", [128, 1], F32)
    nfgb = sb("nfgb", [16, 1], F32)
    rsel_sb = sb("rsel_sb", [128, 2], F32)
    ident_bf = sb("ident_bf", [128, 128], BF16)
    K.dma("pool", ident_bf[:], ident_in, writes=["ident"])

    K.op("dve", lambda e: e.memset(ones_bf[:], 1.0), writes=["ones"])
    K.op("dve", lambda e: e.memset(ones_f[:], 1.0), writes=["onesf"])
    K.dma("sp", cact[:], cT, writes=["cact"])
    K.dma("sp", bias_all[:, 0:144], ada_bT, writes=["bias_all"])
    K.dma("sp", bias_all[:, 144:160], kv_ada_bT, writes=["bias_all"])
    K.dma("sp", ng_all[:, 0:48], norm_gT, writes=["ng_all"])
    K.dma("sp", ng_all[:, 48:56], kv_norm_gT, writes=["ng_all"])
    K.dma("sp", ng_all[:, 56:64], final_gT, writes=["ng_all"])
    K.dma("sp", lam_sb[:], a_lam, writes=["lam"])
    K.dma("sp", gsub[:], a_subg, writes=["gsub"])
    K.dma("sp", nfgb[:], fgate_bT, writes=["nfgb"])
    K.dma("sp", rsel_sb[:], rsel, writes=["rsel"])
    K.op("act", lambda e: e.activation(out=cact[:], in_=cact[:], func=AF.Silu), reads=["cact"], writes=["cact"])
    K.op("dve", lambda e: e.tensor_scalar(out=nfgb[:], in0=nfgb[:], scalar1=-1.0, scalar2=0.0, op0=ALU.mult, op1=ALU.add),
         reads=["nfgb"], writes=["nfgb"])
    linit = 0.8 - 0.6 * math.exp(-0.3 * 0)
    K.op("dve", lambda e: e.tensor_scalar(out=gsub[:], in0=gsub[:], scalar1=1.0 - linit, scalar2=0.0, op0=ALU.mult, op1=ALU.add),
         reads=["gsub"], writes=["gsub"])
    K.op("dve", lambda e: e.tensor_tensor(out=lam_sb[:, 0:64], in0=lam_sb[:, 0:64], in1=lam_sb[:, 64:128], op=ALU.mult),
         reads=["lam"], writes=["lam"])
    K.op("dve", lambda e: e.tensor_tensor(out=lam_sb[:, 128:192], in0=lam_sb[:, 128:192], in1=lam_sb[:, 192:256], op=ALU.mult),
         reads=["lam"], writes=["lam"])
    K.op("dve", lambda e: e.reduce_sum(out=lam_t[:, 0:1], in_=lam_sb[:, 0:64], axis=mybir.AxisListType.X),
         reads=["lam"], writes=["lamt"])
    K.op("dve", lambda e: e.reduce_sum(out=lam_t[:, 1:2], in_=lam_sb[:, 128:192], axis=mybir.AxisListType.X),
         reads=["lam"], writes=["lamt"])
    K.op("act", lambda e: e.activation(out=lam_t[:, 2:4], in_=lam_t[:, 0:2], func=AF.Exp), reads=["lamt"], writes=["lamt"])
    K.op("dve", lambda e: e.tensor_tensor(out=lam_t[:, 4:5], in0=lam_t[:, 3:4], in1=lam_t[:, 2:3], op=ALU.subtract),
         reads=["lamt"], writes=["lamt"])
    K.op("dve", lambda e: e.tensor_scalar(out=lam_t[:, 5:6], in0=lam_t[:, 4:5], scalar1=-linit, scalar2=0.0, op0=ALU.add, op1=ALU.add),
         reads=["lamt"], writes=["lamt"])
    K.op("pe", lambda e: e.matmul(ps[7][:, 0:1], lhsT=ones_f[0:1, :], rhs=lam_t[0:1, 5:6], start=True, stop=True),
         reads=["lamt", "onesf"], writes=[("ps", 7)])
    K.op("dve", lambda e: e.tensor_copy(out=nlam[:], in_=ps[7][:, 0:1]), reads=[("ps", 7)], writes=["nlam"])

    with ExitStack() as st:
        wb = [sb("modw%d" % i, [128, 8, 1152], F32, st) for i in range(2)]
        jobs = []
        for l in range(2):
            for fb in range(8):
                jobs.append((ada_w[l], fb * 1152, 1152, l * 72 + fb * 9, ada_wk))
        for fb in range(2):
            jobs.append((kv_ada_w, fb * 1024, 1024, 144 + fb * 8, kv_ada_wk))
        for ji, (w, f0, fw, col0, wkey) in enumerate(jobs):
            b = ji % 2
            wv = w.rearrange("(k p) f -> p k f", p=128)
            K.dma("sp", wb[b][:, :, :fw], wv[:, :, f0:f0 + fw], reads=[wkey] if wkey else [], writes=[("modw", b)])
            nch = fw // 128
            pb = ji % 2
            for j in range(nch):
                for k in range(8):
                    K.op("pe", lambda e, b=b, j=j, k=k, pb=pb: e.matmul(
                        ps[pb][:, j:j + 1], lhsT=wb[b][:, k, j * 128:(j + 1) * 128], rhs=cact[:, k:k + 1],
                        start=(k == 0), stop=(k == 7)),
                        reads=[("modw", b), "cact"], writes=[("ps", pb)], sig=(k == 7 and j == nch - 1))
            K.op("dve", lambda e, pb=pb, col0=col0, nch=nch: e.tensor_tensor(
                out=modraw[:, col0:col0 + nch], in0=ps[pb][:, 0:nch], in1=bias_all[:, col0:col0 + nch], op=ALU.add),
                reads=[("ps", pb), "bias_all"], writes=["modraw"])
        for l in range(2):
            for s_ in range(3):
                ni = l * 4 + s_
                base = l * 72 + s_ * 24
                K.op("dve", lambda e, ni=ni, base=base: e.tensor_copy(out=modB[:, ni, :], in_=modraw[:, base:base + 8]),
                     reads=["modraw"], writes=["mod"])
                K.op("dve", lambda e, ni=ni, base=base, l=l, s_=s_: e.scalar_tensor_tensor(
                    out=modA[:, ni, :], in0=modraw[:, base + 8:base + 16], scalar=1.0,
                    in1=ng_all[:, (l * 3 + s_) * 8:(l * 3 + s_) * 8 + 8], op0=ALU.add, op1=ALU.mult),
                    reads=["modraw", "ng_all"], writes=["mod"])
                gsc = 1.0 if s_ == 1 else 0.5
                K.op("dve", lambda e, l=l, s_=s_, base=base, gsc=gsc: e.tensor_scalar(
                    out=modG[:, l * 3 + s_, :], in0=modraw[:, base + 16:base + 24], scalar1=gsc, scalar2=0.0, op0=ALU.mult, op1=ALU.add),
                    reads=["modraw"], writes=["mod"])
        K.op("dve", lambda e: e.tensor_copy(out=modB[:, 3, :], in_=modraw[:, 144:152]), reads=["modraw"], writes=["mod"])
        K.op("dve", lambda e: e.scalar_tensor_tensor(out=modA[:, 3, :], in0=modraw[:, 152:160], scalar=1.0,
                                                     in1=ng_all[:, 48:56], op0=ALU.add, op1=ALU.mult),
             reads=["modraw", "ng_all"], writes=["mod"])
        K.op("dve", lambda e: e.tensor_copy(out=modA[:, 7, :], in_=ng_all[:, 56:64]), reads=["ng_all"], writes=["mod"])
        K.barrier()

    def XK(name, b):
        return [(name, b, c) for c in range(8)]

    def norm_mod(xt, xkeys, T, ni, sq, rs, tmp, h, hname, hb, psb, add_shift=True):
        K.op("act", lambda e: e.activation(out=sq[:, :, :T], in_=xt[:, :, :T], func=AF.Square), reads=xkeys, writes=["sq"])
        for c in range(8):
            K.op("pe", lambda e, c=c: e.matmul(ps[psb][:, :T], lhsT=ones_bf[:], rhs=sq[:, c, :T],
                                                start=(c == 0), stop=(c == 7)),
                 reads=["sq", "ones"], writes=[("ps", psb)], sig=(c == 7))
        K.op("act", lambda e: e.activation(out=rs[:, :T], in_=ps[psb][:, :T], func=AF.Sqrt, bias=EPS, scale=1.0 / D),
             reads=[("ps", psb)], writes=["rs"])
        K.op("dve", lambda e: e.reciprocal(out=rs[:, :T], in_=rs[:, :T]), reads=["rs"], writes=["rs"])
        for c in range(8):
            K.op("dve", lambda e, c=c: e.scalar_tensor_tensor(
                out=tmp[:, c, :T], in0=xt[:, c, :T], scalar=modA[:, ni, c:c + 1], in1=rs[:, :T],
                op0=ALU.mult, op1=ALU.mult), reads=[xkeys[c], "rs", "mod"], writes=[("tmp", c)])
            if add_shift:
                K.op("act", lambda e, c=c: e.activation(out=h[:, c, :T], in_=tmp[:, c, :T], func=AF.Identity,
                                                        bias=modB[:, ni, c:c + 1], scale=1.0),
                     reads=[("tmp", c), "mod"], writes=[(hname, hb, c)])

    def load_w(dst, dkey, w, kchunks, per=4):
        w_ap, wkey = w
        wv = w_ap.rearrange("(k p) f -> p k f", p=128)
        for k0 in range(0, kchunks, per):
            k1 = min(kchunks, k0 + per)
            K.dma("pool", dst[:, k0:k1, :], wv[:, k0:k1, :], reads=[wkey] if wkey else [], writes=[dkey])

    def phase_ffn(l, s_, src, dst, pre=None, final=False):
        T = 256
        ntile = NT // T
        ni = l * 4 + (0 if s_ == 0 else 2)
        gi = l * 3 + (0 if s_ == 0 else 2)
        with ExitStack() as st:
            win = sb("win", [128, 8, 2 * FH], BF16, st)
            wout = sb("wout", [128, 22, D], BF16, st)
            xt = [sb("xt%d" % i, [128, 8, T], F32, st) for i in range(2)]
            sq = sb("sq", [128, 8, T], BF16, st)
            rs = sb("rs", [128, T], F32, st)
            tmp = sb("tmp", [128, 8, T], F32, st)
            h = [sb("h%d" % i, [128, 8, T], BF16, st) for i in range(2)]
            sg = [sb("sg%d" % i, [128, T], F32, st) for i in range(2)]
            act = sb("act", [128, 22, T], BF16, st)
            if pre is not None:
                wo = sb("wo", [128, 8, D], BF16, st)
                at1 = sb("at", [128, 8, T], BF16, st)
                at = [at1, at1]
                load_w(wo, "wo", pre[0], 8)
            load_w(win, "win", ffn_w_in[l, s_], 8, per=1)
            load_w(wout, "wout", ffn_w_out[l, s_], 22, per=6)
            srcv = src.rearrange("(c p) t -> p c t", p=128)
            dstv = dst.rearrange("(c p) t -> p c t", p=128)
            if pre is not None:
                atv = attnT.rearrange("(c p) t -> p c t", p=128)

            def load(i):
                b = i % 2
                K.dma("sp", xt[b][:], srcv[:, :, i * T:(i + 1) * T], writes=XK("xt", b))
                if pre is not None:
                    K.dma("sp", at[b][:], atv[:, :, i * T:(i + 1) * T], writes=[("at", 0)])

            def prep(i):
                b = i % 2
                if pre is not None:
                    for dc in range(8):
                        pb = 5 + dc % 2
                        for c in range(8):
                            K.op("pe", lambda e, b=b, dc=dc, c=c, pb=pb: e.matmul(
                                ps[pb][:, :T], lhsT=wo[:, c, dc * 128:(dc + 1) * 128], rhs=at[b][:, c, :],
                                start=(c == 0), stop=(c == 7)),
                                reads=["wo", ("at", 0)], writes=[("ps", pb)], sig=(c == 7))
                        K.op("dve", lambda e, b=b, dc=dc, pb=pb: e.scalar_tensor_tensor(
                            out=xt[b][:, dc, :], in0=ps[pb][:, :T], scalar=modG[:, pre[1], dc:dc + 1],
                            in1=xt[b][:, dc, :], op0=ALU.mult, op1=ALU.add),
                            reads=[("ps", pb), ("xt", b, dc), "mod"], writes=[("xt", b, dc)])
                norm_mod(xt[b], XK("xt", b), T, ni, sq, rs, tmp, h[b], "h", b, 0)

            def pass1(i):
                b = i % 2
                for fc in range(22):
                    pg, pu = 1 + fc % 2, 3 + fc % 2
                    for k in range(8):
                        K.op("pe", lambda e, fc=fc, k=k, pg=pg: e.matmul(
                            ps[pg][:, :T], lhsT=win[:, k, fc * 128:(fc + 1) * 128], rhs=h[b][:, k, :],
                            start=(k == 0), stop=(k == 7)),
                            reads=["win", ("h", b, k)], writes=[("ps", pg)], sig=(k == 7))
                    for k in range(8):
                        K.op("pe", lambda e, fc=fc, k=k, pu=pu: e.matmul(
                            ps[pu][:, :T], lhsT=win[:, k, FH + fc * 128:FH + (fc + 1) * 128], rhs=h[b][:, k, :],
                            start=(k == 0), stop=(k == 7)),
                            reads=["win", ("h", b, k)], writes=[("ps", pu)], sig=(k == 7))
                    K.op("act", lambda e, fc=fc, pg=pg: e.activation(out=sg[fc % 2][:], in_=ps[pg][:, :T], func=AF.Silu),
                         reads=[("ps", pg)], writes=[("sg", fc % 2)])
                    K.op("dve", lambda e, fc=fc, pu=pu: e.tensor_tensor(out=act[:, fc, :], in0=ps[pu][:, :T],
                                                                        in1=sg[fc % 2][:], op=ALU.mult),
                         reads=[("ps", pu), ("sg", fc % 2)], writes=[("act", fc)])

            def pass2(i):
                b = i % 2
                for dc in range(8):
                    pb = 5 + dc % 2
                    for fc in range(22):
                        K.op("pe", lambda e, dc=dc, fc=fc, pb=pb: e.matmul(
                            ps[pb][:, :T], lhsT=wout[:, fc, dc * 128:(dc + 1) * 128], rhs=act[:, fc, :],
                            start=(fc == 0), stop=(fc == 21)),
                            reads=["wout", ("act", fc)], writes=[("ps", pb)], sig=(fc == 21))
                    K.op("dve", lambda e, dc=dc, pb=pb: e.scalar_tensor_tensor(
                        out=xt[b][:, dc, :], in0=ps[pb][:, :T], scalar=modG[:, gi, dc:dc + 1],
                        in1=xt[b][:, dc, :], op0=ALU.mult, op1=ALU.add),
                        reads=[("ps", pb), ("xt", b, dc), "mod"], writes=[("xt", b, dc)])

            def post(i):
                b = i % 2
                if final:
                    norm_mod(xt[b], XK("xt", b), T, 7, sq, rs, tmp, None, None, None, 7, add_shift=False)
                    K.dma("sp", dstv[:, :, i * T:(i + 1) * T], tmp[:], reads=[("tmp", c) for c in range(8)])
                else:
                    K.dma("sp", dstv[:, :, i * T:(i + 1) * T], xt[b][:], reads=XK("xt", b))

            load(0)
            prep(0)
            for i in range(ntile):
                if i + 1 < ntile:
                    load(i + 1)
                pass1(i)
                if i + 1 < ntile:
                    prep(i + 1)
                pass2(i)
                post(i)
            K.barrier()

    def phase_proj(src, ni, w_ap, nout, fm_outs, tm=None, fgate=False):
        T = 512
        ntile = NT // T
        with ExitStack() as st:
            w = sb("pw", [128, 8, nout], BF16, st)
            xt = [sb("pxt%d" % i, [128, 8, T], F32, st) for i in range(2)]
            sq = sb("psq", [128, 8, T], BF16, st)
            rs = sb("prs", [128, T], F32, st)
            tmp = sb("ptmp", [128, 8, T], F32, st)
            h = sb("ph", [128, 8, T], BF16, st)
            nfm = sum(n for _, n, _, _ in fm_outs)
            stage = sb("pstage", [128, max(nfm, 1), T], BF16, st)
            if tm is not None:
                hd = tm[1]
                if hd == 128:
                    vst = sb("pvst", [128, 4, D], BF16, st)
                else:
                    vst = sb("pvst", [128, 4, 16, 65], BF16, st)
                    K.op("pool", lambda e: e.memset(vst[:], 1.0), writes=[("vst", a_, b_) for a_ in range(4) for b_ in range(2)])
            if fgate:
                fgw = sb("fgw", [128, 8, 16], BF16, st)
                K.dma("pool", fgw[:], fgate_wT.rearrange("p (k h) -> p k h", h=16), writes=["fgw"])
                e1 = sb("fge1", [16, T], F32, st)
                lgs = sb("fglg", [16, T], F32, st)
            load_w(w, "pw", w_ap, 8, per=2)
            srcv = src.rearrange("(c p) t -> p c t", p=128)

            K.dma("sp", xt[0][:], srcv[:, :, 0:T], writes=XK("pxt", 0))
            for i in range(ntile):
                b = i % 2
                if i + 1 < ntile:
                    K.dma("sp", xt[1 - b][:], srcv[:, :, (i + 1) * T:(i + 2) * T], writes=XK("pxt", 1 - b))
                norm_mod(xt[b], XK("pxt", b), T, ni, sq, rs, tmp, h, "ph", 0, 0)
                hk = [("ph", 0, c) for c in range(8)]
                si = 0
                g = 0
                for (col0, nch, dstd, scale) in fm_outs:
                    for oc in range(nch):
                        pb = 1 + g % 4
                        for k in range(8):
                            K.op("pe", lambda e, k=k, pb=pb, c0=col0 + oc * 128: e.matmul(
                                ps[pb][:, :T], lhsT=w[:, k, c0:c0 + 128], rhs=h[:, k, :],
                                start=(k == 0), stop=(k == 7)),
                                reads=["pw", hk[k]], writes=[("ps", pb)], sig=(k == 7))
                        if g % 2 == 0:
                            K.op("act", lambda e, pb=pb, si=si, scale=scale: e.activation(
                                out=stage[:, si, :], in_=ps[pb][:, :T], func=AF.Copy, scale=scale),
                                reads=[("ps", pb)], writes=[("stage", si)])
                        else:
                            K.op("dve", lambda e, pb=pb, si=si, scale=scale: e.tensor_scalar(
                                out=stage[:, si, :], in0=ps[pb][:, :T], scalar1=scale, scalar2=0.0, op0=ALU.mult, op1=ALU.add),
                                reads=[("ps", pb)], writes=[("stage", si)])
                        si += 1
                        g += 1
                    dv = dstd.rearrange("(c p) t -> p c t", p=128)
                    K.dma("sp", dv[:, :, i * T:(i + 1) * T], stage[:, si - nch:si, :],
                          reads=[("stage", j) for j in range(si - nch, si)])
                if tm is not None:
                    col0, hd, dstd = tm
                    for tb in range(4):
                        for eh in range(2):
                            pb = 1 + g % 4
                            for k in range(8):
                                K.op("pe", lambda e, k=k, pb=pb, tb=tb, c0=col0 + eh * 512: e.matmul(
                                    ps[pb][:, :512], lhsT=h[:, k, tb * 128:(tb + 1) * 128], rhs=w[:, k, c0:c0 + 512],
                                    start=(k == 0), stop=(k == 7)),
                                    reads=["pw", hk[k]], writes=[("ps", pb)], sig=(k == 7))
                            if hd == 128:
                                o_ap = vst[:, tb, eh * 512:(eh + 1) * 512]
                                i_ap = ps[pb][:, :512]
                            else:
                                o_ap = vst[:, tb, eh * 8:(eh + 1) * 8, 0:64]
                                i_ap = ps[pb][:, :512].rearrange("p (h e) -> p h e", e=64)
                            if g % 2 == 0:
                                K.op("act", lambda e, o_ap=o_ap, i_ap=i_ap: e.activation(out=o_ap, in_=i_ap, func=AF.Copy),
                                     reads=[("ps", pb)], writes=[("vst", tb, eh)])
                            else:
                                K.op("dve", lambda e, o_ap=o_ap, i_ap=i_ap: e.tensor_copy(out=o_ap, in_=i_ap),
                                     reads=[("ps", pb)], writes=[("vst", tb, eh)])
                            g += 1
                        blk = i * 4 + tb
                        if hd == 128:
                            dv = dstd.rearrange("(h p) (b e) -> p h b e", p=128, e=128)
                            K.dma("sp", dv[:, :, blk, :], vst[:, tb, :].rearrange("p (h e) -> p h e", e=128), reads=[("vst", tb, 0), ("vst", tb, 1)])
                        else:
                            dv = dstd.rearrange("(h p) (b e) -> p h b e", p=128, e=65)
                            K.dma("sp", dv[:, :, blk, :], vst[:, tb, :, :], reads=[("vst", tb, 0), ("vst", tb, 1)])
                if fgate:
                    for k in range(8):
                        K.op("pe", lambda e, k=k: e.matmul(ps[5][0:16, :T], lhsT=fgw[:, k, :], rhs=h[:, k, :],
                                                            start=(k == 0), stop=(k == 7)),
                             reads=["fgw", hk[k]], writes=[("ps", 5)], sig=(k == 7))
                    K.op("act", lambda e: e.activation(out=e1[:], in_=ps[5][0:16, :T], func=AF.Exp, bias=nfgb[:, 0:1], scale=-1.0),
                         reads=[("ps", 5), "nfgb"], writes=["e1"])
                    K.op("act", lambda e: e.activation(out=e1[:], in_=e1[:], func=AF.Ln, bias=1.0, scale=1.0),
                         reads=["e1"], writes=["e1"])
                    K.op("dve", lambda e: e.tensor_scalar(out=lgs[:], in0=e1[:], scalar1=-1.0, scalar2=0.0, op0=ALU.mult, op1=ALU.add),
                         reads=["e1"], writes=["lgs"])
                    K.dma("sp", lf_loc[:, i * T:(i + 1) * T], lgs[:], reads=["lgs"])
            K.barrier()

    def setup_A(st):
        oh = sb("oh", [128, 2, 32, 128], F32, st)
        ngc = sb("ngc", [128, 128], F32, st)
        Tn = sb("Tn", [128, 32, 8], F32, st)
        DP = sb("DP", [128, 8, 2, 128], F32, st)
        cf = sb("cf", [128, NZ * 4, 3], F32, st)
        K.dma("sp", oh[:], onehot.rearrange("p (a b q) -> p a b q", a=2, b=32), writes=["oh"])
        K.dma("sp", ngc[:], negc, writes=["ngc"])
        K.dma("sp", Tn[:], relb.rearrange("p (b h) -> p b h", h=8), writes=["Tn"])
        K.dma("sp", cf[:], coefA.rearrange("p (z c) -> p z c", c=3), writes=["cf"])
        for bk in range(31):
            K.op("dve", lambda e, bk=bk: e.tensor_tensor(out=Tn[:, bk, :], in0=Tn[:, bk, :], in1=Tn[:, 31, :],
                                                          op=ALU.subtract), reads=["Tn"], writes=["Tn"])
        for hh in range(8):
            K.op("dve", lambda e, hh=hh: e.tensor_copy(out=DP[:, hh, 0, :], in_=ngc[:]), reads=["ngc"], writes=["DP"])
            K.op("dve", lambda e, hh=hh: e.memset(DP[:, hh, 1, :], 0.0), writes=["DP"])
            for a_ in range(2):
                for bk in range(31):
                    K.op("dve", lambda e, hh=hh, a_=a_, bk=bk: e.scalar_tensor_tensor(
                        out=DP[:, hh, a_, :], in0=oh[:, a_, bk, :], scalar=Tn[:, bk, hh:hh + 1],
                        in1=DP[:, hh, a_, :], op0=ALU.mult, op1=ALU.add),
                        reads=["oh", "Tn", "DP"], writes=["DP"])
        return oh, ngc, Tn, DP, cf

    def phase_attn(layer, preA=None):
        isA = layer == 0
        NH = 8 if isA else 16
        KC = 128 if isA else 70
        nmap = 2 if isA else 1
        VW = 128 if isA else 65
        TQ = 512
        ntile = NT // TQ
        with ExitStack() as st:
            Kt = [sb("Kt%d" % i, [KC, NR * NT], BF16, st) for i in range(2)]
            Qt = [sb("Qt%d" % i, [KC, NT], BF16, st) for i in range(2)]
            Vt = [sb("Vt%d" % i, [128, NR * NBLK, VW], BF16, st) for i in range(2)]
            rz = sb("rz", [128, TQ], F32, st)
            ast = [sb("ast%d" % i, [128, TQ], BF16, st) for i in range(2)]
            if isA:
                Mk = [sb("Mk%d" % i, [128, NZ, TQ], BF16, st) for i in range(2)]
                oh, ngc, Tn, DP, cf = preA
                t1 = sb("t1", [128, 128], F32, st)
                oA = sb("oA", [128, TQ], F32, st)
                on = sb("on", [128, TQ], F32, st)
                osq = sb("osq", [128, TQ], BF16, st)
                ors = sb("ors", [128, TQ], F32, st)
            else:
                Mk1 = sb("MkB", [128, NZ, TQ], BF16, st)
                Mk = [Mk1, Mk1]
                K.dma("pool", Mk1[:], maskB.rearrange("p (z q) -> p z q", q=TQ), writes=[("Mk", 0), ("Mk", 1)])
                bc = sb("bcB", [64, TQ], F32, st)
                for i in range(2):
                    K.op("pool", lambda e, i=i: e.memset(Kt[i][64:70, :], 1.0), writes=[("Kt", i)])
                    K.op("pool", lambda e, i=i: e.memset(Qt[i][64:70, :], 1.0), writes=[("Qt", i)])
                    K.op("pool", lambda e, i=i: e.memset(Vt[i][:, :, 64:65], 1.0), writes=[("Vt", i)])

            def load_head(hh):
                b = hh % 2
                if isA:
                    for r in range(NR):
                        for hf in range(2):
                            ro = ((hh * 2 + hf) * NR + r) * 64
                            K.dma("sp", Kt[b][hf * 64:(hf + 1) * 64, r * NT:(r + 1) * NT], kT_all[ro:ro + 64, :],
                                  writes=[("Kt", b)])
                            K.dma("sp", Vt[b][hf * 64:(hf + 1) * 64, r * NBLK:(r + 1) * NBLK, :],
                                  v_all[ro:ro + 64, :].rearrange("p (b e) -> p b e", e=128),
                                  writes=[("Vt", b)])
                    K.dma("sp", Qt[b][:], qT[hh * 128:(hh + 1) * 128, :], writes=[("Qt", b)])
                    for z in range(NZ):
                        for c in range(4):
                            zi = z * 4 + c
                            K.op("dve", lambda e, hh=hh, zi=zi: e.tensor_scalar(
                                out=t1[:], in0=DP[:, hh, 1, :], scalar1=cf[:, zi, 1:2], scalar2=cf[:, zi, 2:3],
                                op0=ALU.mult, op1=ALU.add), reads=["DP", "cf"], writes=["t1"])
                            K.op("dve", lambda e, hh=hh, zi=zi, z=z, c=c, b=b: e.scalar_tensor_tensor(
                                out=Mk[b][:, z, c * 128:(c + 1) * 128], in0=DP[:, hh, 0, :], scalar=cf[:, zi, 0:1],
                                in1=t1[:], op0=ALU.mult, op1=ALU.add), reads=["DP", "cf", "t1"], writes=[("Mk", b)])
                else:
                    for r in range(NR):
                        ro = (hh * NR + r) * 64
                        K.dma("sp", Kt[b][0:64, r * NT:(r + 1) * NT], kT_all[ro:ro + 64, :],
                              writes=[("Kt", b)])
                        K.dma("sp", Vt[b][:, r * NBLK:(r + 1) * NBLK, :],
                              vb_all[(hh * NR + r) * 128:(hh * NR + r + 1) * 128, :].rearrange("p (b e) -> p b e", e=65),
                              writes=[("Vt", b)])
                    K.dma("sp", Kt[b][67:70, :], FkT[hh * 3:(hh + 1) * 3, :], writes=[("Kt", b)])
                    K.dma("sp", Qt[b][0:64, :], qT[hh * 64:(hh + 1) * 64, :], writes=[("Qt", b)])
                    K.dma("sp", Qt[b][64:67, :], FqT[hh * 3:(hh + 1) * 3, :], writes=[("Qt", b)])

            it = [0]
            NPT = 4
            LOOKG = 2
            pt2 = [sb("pt2_%d" % i, [128, 2 * TQ], BF16, st) for i in range(NPT)]

            def attend(hh, j, m):
                b = hh % 2
                a_i = it[0] % 2
                it[0] += 1
                if isA:
                    po, pz = 6, 7
                else:
                    po, pz = 6 + a_i, None
                nz, zn = [], []
                for r in range(NR):
                    for i in range(4 * j + 4):
                        if i >= 4 * j:
                            zn.append((r * NBLK + i, r * 4 + (i - 4 * j)))
                        elif isA and r == NR - 1 and i == 4 * j - 1:
                            zn.append((r * NBLK + i, NZ - 1))
                        else:
                            nz.append((r * NBLK + i, None))
                groups = [nz[x:x + 2] for x in range(0, len(nz), 2)] + [zn[x:x + 2] for x in range(0, len(zn), 2)]
                ng = len(groups)
                nkb = len(nz) + len(zn)
                if isA:
                    k0, k1 = m * 64, m * 64 + 64
                else:
                    k0, k1 = 0, 70
                cnt = [0]
                for gi in range(ng + LOOKG):
                    if gi < ng:
                        grp = groups[gi]
                        s2 = gi % 3
                        W = TQ * len(grp)
                        for e_, (kb, z) in enumerate(grp):
                            bk = 2 * s2 + e_
                            zone = z is not None
                            K.op("pe", lambda e, kb=kb, bk=bk, zone=zone: e.matmul(
                                ps[bk][:, :TQ], lhsT=Kt[b][k0:k1, kb * 128:(kb + 1) * 128],
                                rhs=Qt[b][k0:k1, j * TQ:(j + 1) * TQ], start=True, stop=not zone),
                                reads=[("Kt", b), ("Qt", b)], writes=[("ps", bk)], sig=(e_ == len(grp) - 1) and not zone)
                            if zone:
                                K.op("pe", lambda e, bk=bk, z=z: e.matmul(
                                    ps[bk][:, :TQ], lhsT=ident_bf[:], rhs=Mk[b][:, z, :], start=False, stop=True),
                                    reads=["ident", ("Mk", b)], writes=[("ps", bk)], sig=(e_ == len(grp) - 1))
                        bks = [("ps", 2 * s2 + e_) for e_ in range(len(grp))]
                        if True:
                            K.op("act", lambda e, gi=gi, s2=s2, W=W: e.activation(
                                out=pt2[gi % NPT][:, :W], in_=psbig[:, 2 * s2 * TQ:2 * s2 * TQ + W], func=AF.Exp),
                                reads=bks, writes=[("pt2", gi % NPT)])
                    if gi >= LOOKG:
                        gp = gi - LOOKG
                        grp = groups[gp]
                        for e_, (kb, z) in enumerate(grp):
                            u = cnt[0]
                            cnt[0] += 1
                            K.op("pe", lambda e, kb=kb, u=u, e_=e_, gp=gp: e.matmul(
                                ps[po][0:VW, :TQ], lhsT=Vt[b][:, kb, :], rhs=pt2[gp % NPT][:, e_ * TQ:(e_ + 1) * TQ],
                                start=(u == 0), stop=(u == nkb - 1)),
                                reads=[("Vt", b), ("pt2", gp % NPT)], writes=[("ps", po)],
                                sig=(not isA) and e_ == len(grp) - 1)
                            if isA:
                                K.op("pe", lambda e, u=u, e_=e_, gp=gp: e.matmul(
                                    ps[pz][:, :TQ], lhsT=ones_bf[:], rhs=pt2[gp % NPT][:, e_ * TQ:(e_ + 1) * TQ],
                                    start=(u == 0), stop=(u == nkb - 1)),
                                    reads=["ones", ("pt2", gp % NPT)], writes=[("ps", pz)], sig=(e_ == len(grp) - 1))
                return po, pz

            def epilogue_A(hh, j, m, po, pz, scr):
                K.op("act", lambda e: e.activation(out=rz[:], in_=ps[pz][:, :TQ], func=AF.Ln), reads=[("ps", pz)], writes=["rz"])
                K.op("act", lambda e: e.activation(out=rz[:], in_=rz[:], func=AF.Exp, scale=-1.0), reads=["rz"], writes=["rz"])
                if m == 0:
                    K.op("dve", lambda e: e.tensor_tensor(out=oA[:], in0=ps[po][:, :TQ], in1=rz[:], op=ALU.mult),
                         reads=[("ps", po), "rz"], writes=["oA"])
                    return
                K.op("dve", lambda e: e.tensor_tensor(out=on[:], in0=ps[po][:, :TQ], in1=rz[:], op=ALU.mult),
                     reads=[("ps", po), "rz"], writes=["on"])
                K.op("dve", lambda e: e.scalar_tensor_tensor(out=on[:], in0=on[:], scalar=nlam[:, 0:1], in1=oA[:],
                                                             op0=ALU.mult, op1=ALU.add),
                     reads=["on", "oA", "nlam"], writes=["on"])
                K.op("act", lambda e: e.activation(out=osq[:], in_=on[:], func=AF.Square), reads=["on"], writes=["osq"])
                K.op("pe", lambda e: e.matmul(ps[scr][:, :TQ], lhsT=ones_bf[:], rhs=osq[:], start=True, stop=True),
                     reads=["ones", "osq"], writes=[("ps", scr)])
                K.op("act", lambda e: e.activation(out=ors[:], in_=ps[scr][:, :TQ], func=AF.Ln, bias=EPS, scale=1.0 / 128),
                     reads=[("ps", scr)], writes=["ors"])
                K.op("act", lambda e: e.activation(out=ors[:], in_=ors[:], func=AF.Exp, scale=-0.5), reads=["ors"], writes=["ors"])
                sb_ = j % 2
                K.op("dve", lambda e: e.scalar_tensor_tensor(out=ast[sb_][:], in0=on[:], scalar=gsub[:, 0:1], in1=ors[:],
                                                             op0=ALU.mult, op1=ALU.mult),
                     reads=["on", "ors", "gsub"], writes=[("ast", sb_)])
                K.dma("sp", attnT[hh * 128:(hh + 1) * 128, j * TQ:(j + 1) * TQ], ast[sb_][:], reads=[("ast", sb_)])

            def epilogue_B(hh, j, po):
                K.op("act", lambda e: e.activation(out=rz[64:65, :], in_=ps[po][64:65, :TQ], func=AF.Ln), reads=[("ps", po)], writes=["rz"])
                K.op("act", lambda e: e.activation(out=rz[64:65, :], in_=rz[64:65, :], func=AF.Exp, scale=-1.0), reads=["rz"], writes=["rz"])
                K.op("pe", lambda e: e.matmul(ps[0][0:64, :TQ], lhsT=ones_f[64:65, 0:64], rhs=rz[64:65, :], start=True, stop=True),
                     reads=["onesf", "rz"], writes=[("ps", 0)])
                K.op("act", lambda e: e.activation(out=bc[:], in_=ps[0][0:64, :TQ], func=AF.Copy), reads=[("ps", 0)], writes=["bc"])
                sb_ = j % 2
                K.op("dve", lambda e: e.tensor_tensor(out=ast[sb_][0:64, :], in0=ps[po][0:64, :TQ], in1=bc[:], op=ALU.mult),
                     reads=[("ps", po), "bc"], writes=[("ast", sb_)])
                K.dma("sp", attnT[hh * 64:(hh + 1) * 64, j * TQ:(j + 1) * TQ], ast[sb_][0:64, :], reads=[("ast", sb_)])

            load_head(0)
            for hh in range(NH):
                if hh + 1 < NH:
                    load_head(hh + 1)
                for j in range(ntile):
                    prev_pz = None
                    for m in range(nmap):
                        po, pz = attend(hh, j, m)
                        if isA:
                            epilogue_A(hh, j, m, po, pz, 0)
                            prev_pz = pz
                        else:
                            epilogue_B(hh, j, po)
            K.barrier()

    def phase_F():
        with ExitStack() as st:
            bufA = sb("fA", [16, S], F32, st)
            bufB = sb("fB", [16, S], F32, st)
            Fg = sb("fFg", [16, S], F32, st)
            FkS = sb("fFkS", [16, 3, NR * NT], BF16, st)
            Fq = sb("fFq", [16, NT], F32, st)
            FqS = sb("fFqS", [16, 3, NT], BF16, st)
            Lv = bufA[:].rearrange("h (i r p) -> h i r p", r=NR, p=128)
            for r in range(NR):
                K.dma("sp", Lv[:, :, r, :], lf_all[r * 16:(r + 1) * 16, :].rearrange("h (i p) -> h i p", p=128),
                      writes=["fA"])
            K.op("dve", lambda e: e.memset(bufB[:], 1.0), writes=["fB"])
            K.op("dve", lambda e: e.tensor_tensor_scan(out=Fg[:], data0=bufB[:], data1=bufA[:], initial=0.0,
                                                       op0=ALU.mult, op1=ALU.add),
                 reads=["fA", "fB"], writes=["Fg"])
            Fv = Fg[:].rearrange("h (i r p) -> h i r p", r=NR, p=128)

            def split3(src_ap, dst3, shp, r1, r1key, negate):
                sgn = -1.0 if negate else 1.0
                K.op("dve", lambda e: e.tensor_scalar(out=r1, in0=src_ap, scalar1=sgn, scalar2=0.0, op0=ALU.mult, op1=ALU.add),
                     reads=["Fg", "Fq"], writes=[r1key])
                for t in range(3):
                    K.op("dve", lambda e, t=t: e.tensor_copy(out=dst3(t), in_=r1), reads=[r1key], writes=["split"])
                    if t < 2:
                        K.op("dve", lambda e, t=t: e.tensor_tensor(out=r1, in0=r1, in1=dst3(t), op=ALU.subtract),
                             reads=[r1key, "split"], writes=[r1key])

            for r in range(NR):
                r1 = bufB[:, 0:NT].rearrange("h (i p) -> h i p", p=128)
                split3(Fv[:, :, r, :], lambda t, r=r: FkS[:, t, r * NT:(r + 1) * NT].rearrange("h (i p) -> h i p", p=128),
                       None, r1, "fB", True)
            if NR == 2:
                Fq3 = Fq[:].rearrange("h (i p) -> h i p", p=128)
                K.op("dve", lambda e: e.tensor_scalar(out=Fq3, in0=Fv[:, :, 0, :], scalar1=rsel_sb[0:16, 0:1], scalar2=0.0,
                                                      op0=ALU.mult, op1=ALU.add), reads=["Fg", "rsel"], writes=["Fq"])
                K.op("dve", lambda e: e.scalar_tensor_tensor(out=Fq3, in0=Fv[:, :, 1, :], scalar=rsel_sb[0:16, 1:2], in1=Fq3,
                                                             op0=ALU.mult, op1=ALU.add),
                     reads=["Fg", "rsel", "Fq"], writes=["Fq"])
                fq_src = Fq[:]
            else:
                fq_src = Fg[:]
            split3(fq_src, lambda t: FqS[:, t, :], None, bufA[:, 0:NT], "fA", False)
            K.dma("sp", FkT.rearrange("(h t) k -> h t k", t=3), FkS[:], reads=["split"])
            K.dma("sp", FqT.rearrange("(h t) k -> h t k", t=3), FqS[:], reads=["split"])
            K.barrier()

    def done(name):
        return stop_after == name

    def finish():
        K.barrier(cc=True)
        for dst_, src_ in dbg_copies:
            if any(src_ is w_ for w_ in coll_written):
                K.dma("sp", dst_, src_)
        K.barrier(cc=True)
        es.close()
        return nc

    phase_ffn(0, 0, xT_in, x_s)
    if done("ffn00"):
        return finish()
    phase_proj(x_s, 1, (a_w_qkv, a_w_qkvk), 3 * D, [(0, 8, qT, 0.125), (D, 8, kT_loc, 1.0)], tm=(2 * D, 128, v_loc))
    K.collective_rows(kT_loc, kT_all, 64)
    K.collective_rows(v_loc, v_all, 64)
    coll_written.extend([kT_loc, kT_all, v_loc, v_all])
    stA = ExitStack()
    preA = setup_A(stA)
    K.barrier(cc=True)
    if done("projA"):
        stA.close()
        return finish()
    phase_attn(0, preA)
    stA.close()
    if done("attnA"):
        return finish()
    phase_ffn(0, 1, x_s, x_s, pre=((a_w_o, a_w_ok), 1))
    if done("ffn01"):
        return finish()
    phase_proj(x_s, 3, (kv_w, kv_wk), 2 * D, [(0, 8, kT_loc, 1.0)], tm=(D, 64, vb_loc), fgate=True)
    K.collective_rows(kT_loc, kT_all, 64)
    K.collective_rows(vb_loc, vb_all, 128)
    K.collective_rows(lf_loc, lf_all, 16)
    coll_written.extend([vb_loc, vb_all, lf_loc, lf_all])
    K.barrier(cc=True)
    phase_F()
    if done("projKV"):
        return finish()
    phase_ffn(1, 0, x_s, x_s)
    phase_proj(x_s, 5, (b_w_q, b_w_qk), D, [(0, 8, qT, 0.125)])
    if done("projQ"):
        return finish()
    phase_attn(1)
    if done("attnB"):
        return finish()
    phase_ffn(1, 1, x_s, outT, pre=((b_w_o, b_w_ok), 4), final=True)
    return finish()


def _t5_bucket(n):
    n = np.maximum(n, 0)
    nf = np.maximum(n, 1).astype(np.float32)
    large = 16 + (np.log(nf / 16) / math.log(128 / 16) * 16).astype(np.int32)
    large = np.minimum(large, 31)
    return np.where(n < 16, n, large)


def _fm(v):
    return np.ascontiguousarray(v.reshape(-1, 128).T)


def make_in_maps(inp):
    f32 = np.float32
    x = np.asarray(inp["x"], f32)
    k = np.arange(128)[:, None]
    q = np.arange(128)[None, :]
    oh = np.zeros((128, 2, 32, 128), f32)
    relD = q - k
    bD = _t5_bucket(relD)
    relP = 128 + q - k
    bP = _t5_bucket(relP)
    for b in range(32):
        oh[:, 0, b, :] = ((bD == b) & (relD >= 0))
        oh[:, 1, b, :] = (bP == b)
    negc = np.where(relD < 0, NEG, 0.0).astype(f32)
    big = {
        "ada_w": np.asarray(inp["ada_w"], f32).reshape(2 * D, 9 * D),
        "kv_ada_w": np.asarray(inp["kv_ada_w"], f32),
        "a_w_qkv": np.asarray(inp["a_w_qkv"][0], f32),
        "a_w_o": np.asarray(inp["a_w_o"][0], f32),
        "kv_w": np.asarray(inp["kv_w"], f32),
        "b_w_q": np.asarray(inp["b_w_q"][0], f32),
        "b_w_o": np.asarray(inp["b_w_o"][0], f32),
    }
    for l_ in range(2):
        for s_ in range(2):
            big["ffn_w_in_%d%d" % (l_, s_)] = np.asarray(inp["ffn_w_in"][l_, s_], f32)
            big["ffn_w_out_%d%d" % (l_, s_)] = np.asarray(inp["ffn_w_out"][l_, s_], f32)
    shared = {
        "ada_bT": np.concatenate([_fm(np.asarray(inp["ada_b"][l], f32)) for l in range(2)], 1),
        "norm_gT": np.concatenate([_fm(np.asarray(inp["norm_g"][l, s], f32)) for l in range(2) for s in range(3)], 1),
        "a_lam": np.ascontiguousarray(np.asarray(inp["a_lambda"][0], f32).reshape(1, 256)),
        "a_subg": np.ascontiguousarray(np.asarray(inp["a_subln_g"][0], f32).reshape(128, 1)),
        "relb": np.ascontiguousarray(np.broadcast_to(np.asarray(inp["rel_bias"], f32).reshape(1, 256), (128, 256))),
        "kv_ada_bT": _fm(np.asarray(inp["kv_ada_b"], f32)),
        "kv_norm_gT": _fm(np.asarray(inp["kv_norm_g"], f32)),
        "fgate_wT": np.ascontiguousarray(np.asarray(inp["fgate_w"], f32).reshape(8, 128, 16).transpose(1, 0, 2).reshape(128, 128)),
        "fgate_bT": np.ascontiguousarray(np.asarray(inp["fgate_b"], f32).reshape(16, 1)),
        "final_gT": _fm(np.asarray(inp["final_g"], f32)),
        "onehot": oh.reshape(128, -1),
        "negc": negc,
        "ident": np.eye(128, dtype=f32),
    }
    per_rank = []
    for r in range(NR):
        coef = np.zeros((128, NZ, 4, 3), f32)
        mB = np.zeros((128, NZ, 4, 128), f32)
        for rp, a in [(rp_, a_) for rp_ in range(NR) for a_ in range(4)] + [(NR - 1, -1)]:
            if True:
                z = rp * 4 + a if a >= 0 else NZ - 1
                for c in range(4):
                    diff = NR * (a - c) + (rp - r)
                    if diff > 0:
                        coef[:, z, c, 2] = NEG
                        mB[:, z, c, :] = NEG
                    elif diff == 0:
                        coef[:, z, c, 0] = 1.0
                        mB[:, z, c, :] = negc
                    elif diff == -1:
                        coef[:, z, c, 1] = 1.0
        rs = np.zeros((128, 2), f32)
        rs[:, 0] = 1.0 - r
        rs[:, 1] = r
        per_rank.append({"coefA": coef.reshape(128, -1), "maskB": mB.reshape(128, -1), "rsel": rs})
    maps = []
    for core in range(NCORE):
        b, r = core // NR, core % NR
        xb = x[b].reshape(NBLK, NR, 128, D)[:, r].reshape(NT, D)
        m = dict(shared)
        m.update(per_rank[r])
        for wn, wa in big.items():
            if WSHARD:
                rows = wa.shape[0] // NCORE
                m[wn] = np.ascontiguousarray(wa[core * rows:(core + 1) * rows])
            else:
                m[wn] = np.ascontiguousarray(wa)
        m["xT"] = np.ascontiguousarray(xb.T)
        m["cT"] = _fm(np.asarray(inp["c"][b], f32))
        maps.append(m)
    return maps


def assemble(outs):
    y = np.zeros((NB, S, D), np.float32)
    for core in range(NCORE):
        b, r = core // NR, core % NR
        o = np.asarray(outs[core]).T.reshape(NBLK, 128, D)
        y[b].reshape(NBLK, NR, 128, D)[:, r] = o
    return y


def kernel(**inputs):
    nc = build()
    in_maps = make_in_maps(inputs)
    res = run_bass_kernel_spmd(nc, in_maps, core_ids=list(range(NCORE)))
    return assemble([res.results[i]["outT"] for i in range(NCORE)])
```

```python
import math
from contextlib import ExitStack

import numpy as np
import concourse.bass as bass
import concourse.mybir as mybir
from concourse.bass_utils import run_bass_kernel_spmd

F32 = mybir.dt.float32
BF16 = mybir.dt.bfloat16
AF = mybir.ActivationFunctionType
ALU = mybir.AluOpType

D = 1024
S = 8192
NB = 4
NR = 2
NT = S // NR
NCORE = 8
FH = 2816
EPS = 1e-6
NEG = -1.0e4
NBLK = NT // 128
NZ = NR * 4 + 1
SAME_SYNC = True
WSHARD = False


class KB:
    def __init__(self, nc, es):
        self.nc = nc
        self.E = {"pe": nc.tensor, "act": nc.scalar, "dve": nc.vector, "pool": nc.gpsimd, "sp": nc.sync}
        self.sem = {}
        for e in self.E:
            self.sem[e] = es.enter_context(nc.semaphore("s_" + e))
        self.sem["cc"] = es.enter_context(nc.semaphore("s_cc"))
        self.cnt = {k: 0 for k in self.sem}
        self.KD = 6
        self.dq = {}
        for q in ("sp", "pool"):
            for i in range(self.KD):
                k = ("d", q, i)
                self.sem[k] = es.enter_context(nc.semaphore("d_%s%d" % (q, i)))
                self.cnt[k] = 0
            self.dq[q] = 0
        self.waited = {e: {} for e in self.E}
        self.lastw = {}
        self.readers = {}
        self.nins = 0

    def _wait(self, eng, s, v):
        if v <= 0:
            return
        if s == eng and (eng == "pe" or not SAME_SYNC):
            return
        if self.waited[eng].get(s, 0) >= v:
            return
        self.E[eng].wait_ge(self.sem[s], v)
        self.waited[eng][s] = v

    def _deps(self, eng, reads, writes):
        for k in reads:
            t = self.lastw.get(k)
            if t is not None:
                self._wait(eng, t[0], t[1])
        for k in writes:
            t = self.lastw.get(k)
            if t is not None:
                self._wait(eng, t[0], t[1])
            for s, v in self.readers.get(k, {}).items():
                self._wait(eng, s, v)

    def _record(self, tok, reads, writes):
        for k in reads:
            d = self.readers.setdefault(k, {})
            if d.get(tok[0], 0) < tok[1]:
                d[tok[0]] = tok[1]
        for k in writes:
            self.lastw[k] = tok
            self.readers[k] = {}

    def op(self, eng, fn, reads=(), writes=(), sig=True):
        self._deps(eng, reads, writes)
        ins = fn(self.E[eng])
        self.nins += 1
        if sig:
            self.cnt[eng] += 1
            ins.then_inc(self.sem[eng], 1)
            tok = (eng, self.cnt[eng])
        else:
            tok = (eng, self.cnt[eng] + 1)
        self._record(tok, reads, writes)

    def dma(self, q, out, in_, reads=(), writes=()):
        i = self.dq[q]
        self.dq[q] = (i + 1) % self.KD
        k = ("d", q, i)
        self._wait(q, k, self.cnt[k])
        self._deps(q, reads, writes)
        self.E[q].dma_start(out=out, in_=in_).then_inc(self.sem[k], 16)
        self.nins += 1
        self.cnt[k] += 16
        self._record((k, self.cnt[k]), reads, writes)

    def collective_rows(self, loc, allt, rows_per):
        R = loc.shape[0]
        for c0 in range(0, R, rows_per):
            ci = c0 // rows_per
            self.collective(loc[c0:c0 + rows_per, :], allt[ci * NR * rows_per:(ci + 1) * NR * rows_per, :], [], [])

    def collective(self, ins_ap, outs_ap, reads, writes, allcores=False):
        self._deps("pool", reads, writes)
        if allcores:
            groups = [list(range(NCORE))]
        else:
            groups = [[b * NR + r for r in range(NR)] for b in range(NCORE // NR)]
        self.nc.gpsimd.collective_compute("AllGather", ALU.bypass, replica_groups=groups,
                                          ins=[ins_ap], outs=[outs_ap]).then_inc(self.sem["cc"], 1)
        self.cnt["cc"] += 1
        self._record(("cc", self.cnt["cc"]), reads, writes)

    def barrier(self, cc=False):
        for e in self.E:
            for s in self.sem:
                if s != e and (cc or s != "cc"):
                    self._wait(e, s, self.cnt[s])
        self.lastw = {k: v for k, v in self.lastw.items() if isinstance(k, tuple) and k[0] == "W"}
        self.readers = {}


def build(stop_after=None, debug=False):
    nc = bass.Bass("TRN2", target_bir_lowering=False)
    dbg_kind = "ExternalOutput" if debug else "Internal"

    def din(name, shape, dt=F32):
        return nc.dram_tensor(name, list(shape), dt, kind="ExternalInput").ap()

    dbg_copies = []
    coll_written = []

    def dscr(name, shape, dt=F32, out=False, coll=False):
        if coll:
            t = nc.dram_tensor(name + "_i", list(shape), dt).ap()
            if debug:
                dbg_copies.append((nc.dram_tensor(name, list(shape), dt, kind="ExternalOutput").ap(), t))
            return t
        if out or debug:
            return nc.dram_tensor(name, list(shape), dt, kind="ExternalOutput").ap()
        return nc.dram_tensor(name, list(shape), dt).ap()

    wgather = []

    def dweight(name, R, C):
        if not WSHARD:
            return din(name, [R, C]), None
        sh = din(name, [R // NCORE, C])
        shi = nc.dram_tensor(name + "_shi", [R // NCORE, C], F32).ap()
        full = nc.dram_tensor(name + "_full", [R, C], F32).ap()
        wgather.append((name, sh, shi, full))
        return full, ("W", name)

    xT_in = din("xT", [D, NT])
    cT = din("cT", [128, 8])
    ada_w2, ada_wk = dweight("ada_w", 2 * D, 9 * D)
    ada_w = ada_w2.rearrange("(l d) f -> l d f", l=2)
    ada_bT = din("ada_bT", [128, 144])
    norm_gT = din("norm_gT", [128, 48])
    kv_ada_w, kv_ada_wk = dweight("kv_ada_w", D, 2 * D)
    ffn_w_in = {}
    ffn_w_out = {}
    for l_ in range(2):
        for s_2 in range(2):
            if (l_, s_2) == (0, 1):
                a_w_qkv, a_w_qkvk = dweight("a_w_qkv", D, 3 * D)
                a_w_o, a_w_ok = dweight("a_w_o", D, D)
            if (l_, s_2) == (1, 0):
                kv_w, kv_wk = dweight("kv_w", D, 2 * D)
            if (l_, s_2) == (1, 1):
                b_w_q, b_w_qk = dweight("b_w_q", D, D)
                b_w_o, b_w_ok = dweight("b_w_o", D, D)
            ffn_w_in[l_, s_2] = dweight("ffn_w_in_%d%d" % (l_, s_2), D, 2 * FH)
            ffn_w_out[l_, s_2] = dweight("ffn_w_out_%d%d" % (l_, s_2), FH, D)
    a_lam = din("a_lam", [1, 256])
    a_subg = din("a_subg", [128, 1])
    relb = din("relb", [128, 256])
    kv_ada_bT = din("kv_ada_bT", [128, 16])
    kv_norm_gT = din("kv_norm_gT", [128, 8])
    fgate_wT = din("fgate_wT", [128, 128])
    fgate_bT = din("fgate_bT", [16, 1])
    final_gT = din("final_gT", [128, 8])
    onehot = din("onehot", [128, 2 * 32 * 128])
    negc = din("negc", [128, 128])
    coefA = din("coefA", [128, NZ * 4 * 3])
    maskB = din("maskB", [128, NZ * 512])
    rsel = din("rsel", [128, 2])
    ident_in = din("ident", [128, 128])

    outT = dscr("outT", [D, NT], out=True)
    x_s = dscr("x_s", [D, NT])
    qT = dscr("qT", [D, NT], BF16)
    kT_loc = dscr("kT_loc", [D, NT], BF16, coll=True)
    kT_all = dscr("kT_all", [NR * D, NT], BF16, coll=True)
    v_loc = dscr("v_loc", [D, NT], BF16, coll=True)
    v_all = dscr("v_all", [NR * D, NT], BF16, coll=True)
    attnT = dscr("attnT", [D, NT], BF16)
    vb_loc = dscr("vb_loc", [16 * 128, NBLK * 65], BF16, coll=True)
    vb_all = dscr("vb_all", [NR * 16 * 128, NBLK * 65], BF16, coll=True)
    lf_loc = dscr("lf_loc", [16, NT], coll=True)
    lf_all = dscr("lf_all", [NR * 16, NT], coll=True)
    FkT = dscr("FkT", [16 * 3, NR * NT], BF16)
    FqT = dscr("FqT", [16 * 3, NT], BF16)

    es = ExitStack()
    es.enter_context(nc.allow_low_precision("bf16 matmul operands, fp32 accumulation"))
    K = KB(nc, es)

    uid = [0]

    def sb(name, shape, dt, stack=es):
        uid[0] += 1
        return stack.enter_context(nc.sbuf_tensor("%s_%d" % (name, uid[0]), list(shape), dt))

    psbig = es.enter_context(nc.psum_tensor("psbig", [128, 4096], F32))
    ps = [psbig[:, i * 512:(i + 1) * 512] for i in range(8)]

    for (wname, sh, shi, full) in wgather:
        K.dma("sp", shi, sh, writes=[("Wshi", wname)])
    for (wname, sh, shi, full) in wgather:
        K.collective(shi, full, reads=[("Wshi", wname)], writes=[("W", wname)], allcores=True)

    ones_bf = sb("ones_bf", [128, 128], BF16)
    ones_f = sb("ones_f", [128, 128], F32)
    modA = sb("modA", [128, 8, 8], F32)
    modB = sb("modB", [128, 8, 8], F32)
    modG = sb("modG", [128, 6, 8], F32)
    cact = sb("cact", [128, 8], F32)
    modraw = sb("modraw", [128, 160], F32)
    bias_all = sb("bias_all", [128, 160], F32)
    ng_all = sb("ng_all", [128, 64], F32)
    lam_sb = sb("lam_sb", [1, 256], F32)
    lam_t = sb("lam_t", [1, 8], F32)
    nlam = sb("nlam", [128, 1], F32)
    gsub = sb("gsub", [128, 1], F32)
    nfgb = sb("nfgb", [16, 1], F32)
    rsel_sb = sb("rsel_sb", [128, 2], F32)
    ident_bf = sb("ident_bf", [128, 128], BF16)
    K.dma("pool", ident_bf[:], ident_in, writes=["ident"])

    K.op("dve", lambda e: e.memset(ones_bf[:], 1.0), writes=["ones"])
    K.op("dve", lambda e: e.memset(ones_f[:], 1.0), writes=["onesf"])
    K.dma("sp", cact[:], cT, writes=["cact"])
    K.dma("sp", bias_all[:, 0:144], ada_bT, writes=["bias_all"])
    K.dma("sp", bias_all[:, 144:160], kv_ada_bT, writes=["bias_all"])
    K.dma("sp", ng_all[:, 0:48], norm_gT, writes=["ng_all"])
    K.dma("sp", ng_all[:, 48:56], kv_norm_gT, writes=["ng_all"])
    K.dma("sp", ng_all[:, 56:64], final_gT, writes=["ng_all"])
    K.dma("sp", lam_sb[:], a_lam, writes=["lam"])
    K.dma("sp", gsub[:], a_subg, writes=["gsub"])
    K.dma("sp", nfgb[:], fgate_bT, writes=["nfgb"])
    K.dma("sp", rsel_sb[:], rsel, writes=["rsel"])
    K.op("act", lambda e: e.activation(out=cact[:], in_=cact[:], func=AF.Silu), reads=["cact"], writes=["cact"])
    K.op("dve", lambda e: e.tensor_scalar(out=nfgb[:], in0=nfgb[:], scalar1=-1.0, scalar2=0.0, op0=ALU.mult, op1=ALU.add),
         reads=["nfgb"], writes=["nfgb"])
    linit = 0.8 - 0.6 * math.exp(-0.3 * 0)
    K.op("dve", lambda e: e.tensor_scalar(out=gsub[:], in0=gsub[:], scalar1=1.0 - linit, scalar2=0.0, op0=ALU.mult, op1=ALU.add),
         reads=["gsub"], writes=["gsub"])
    K.op("dve", lambda e: e.tensor_tensor(out=lam_sb[:, 0:64], in0=lam_sb[:, 0:64], in1=lam_sb[:, 64:128], op=ALU.mult),
         reads=["lam"], writes=["lam"])
    K.op("dve", lambda e: e.tensor_tensor(out=lam_sb[:, 128:192], in0=lam_sb[:, 128:192], in1=lam_sb[:, 192:256], op=ALU.mult),
         reads=["lam"], writes=["lam"])
    K.op("dve", lambda e: e.reduce_sum(out=lam_t[:, 0:1], in_=lam_sb[:, 0:64], axis=mybir.AxisListType.X),
         reads=["lam"], writes=["lamt"])
    K.op("dve", lambda e: e.reduce_sum(out=lam_t[:, 1:2], in_=lam_sb[:, 128:192], axis=mybir.AxisListType.X),
         reads=["lam"], writes=["lamt"])
    K.op("act", lambda e: e.activation(out=lam_t[:, 2:4], in_=lam_t[:, 0:2], func=AF.Exp), reads=["lamt"], writes=["lamt"])
    K.op("dve", lambda e: e.tensor_tensor(out=lam_t[:, 4:5], in0=lam_t[:, 3:4], in1=lam_t[:, 2:3], op=ALU.subtract),
         reads=["lamt"], writes=["lamt"])
    K.op("dve", lambda e: e.tensor_scalar(out=lam_t[:, 5:6], in0=lam_t[:, 4:5], scalar1=-linit, scalar2=0.0, op0=ALU.add, op1=ALU.add),
         reads=["lamt"], writes=["lamt"])
    K.op("pe", lambda e: e.matmul(ps[7][:, 0:1], lhsT=ones_f[0:1, :], rhs=lam_t[0:1, 5:6], start=True, stop=True),
         reads=["lamt", "onesf"], writes=[("ps", 7)])
    K.op("dve", lambda e: e.tensor_copy(out=nlam[:], in_=ps[7][:, 0:1]), reads=[("ps", 7)], writes=["nlam"])

    with ExitStack() as st:
        wb = [sb("modw%d" % i, [128, 8, 1152], F32, st) for i in range(2)]
        jobs = []
        for l in range(2):
            for fb in range(8):
                jobs.append((ada_w[l], fb * 1152, 1152, l * 72 + fb * 9, ada_wk))
        for fb in range(2):
            jobs.append((kv_ada_w, fb * 1024, 1024, 144 + fb * 8, kv_ada_wk))
        for ji, (w, f0, fw, col0, wkey) in enumerate(jobs):
            b = ji % 2
            wv = w.rearrange("(k p) f -> p k f", p=128)
            K.dma("sp", wb[b][:, :, :fw], wv[:, :, f0:f0 + fw], reads=[wkey] if wkey else [], writes=[("modw", b)])
            nch = fw // 128
            pb = ji % 2
            for j in range(nch):
                for k in range(8):
                    K.op("pe", lambda e, b=b, j=j, k=k, pb=pb: e.matmul(
                        ps[pb][:, j:j + 1], lhsT=wb[b][:, k, j * 128:(j + 1) * 128], rhs=cact[:, k:k + 1],
                        start=(k == 0), stop=(k == 7)),
                        reads=[("modw", b), "cact"], writes=[("ps", pb)], sig=(k == 7 and j == nch - 1))
            K.op("dve", lambda e, pb=pb, col0=col0, nch=nch: e.tensor_tensor(
                out=modraw[:, col0:col0 + nch], in0=ps[pb][:, 0:nch], in1=bias_all[:, col0:col0 + nch], op=ALU.add),
                reads=[("ps", pb), "bias_all"], writes=["modraw"])
        for l in range(2):
            for s_ in range(3):
                ni = l * 4 + s_
                base = l * 72 + s_ * 24
                K.op("dve", lambda e, ni=ni, base=base: e.tensor_copy(out=modB[:, ni, :], in_=modraw[:, base:base + 8]),
                     reads=["modraw"], writes=["mod"])
                K.op("dve", lambda e, ni=ni, base=base, l=l, s_=s_: e.scalar_tensor_tensor(
                    out=modA[:, ni, :], in0=modraw[:, base + 8:base + 16], scalar=1.0,
                    in1=ng_all[:, (l * 3 + s_) * 8:(l * 3 + s_) * 8 + 8], op0=ALU.add, op1=ALU.mult),
                    reads=["modraw", "ng_all"], writes=["mod"])
                gsc = 1.0 if s_ == 1 else 0.5
                K.op("dve", lambda e, l=l, s_=s_, base=base, gsc=gsc: e.tensor_scalar(
                    out=modG[:, l * 3 + s_, :], in0=modraw[:, base + 16:base + 24], scalar1=gsc, scalar2=0.0, op0=ALU.mult, op1=ALU.add),
                    reads=["modraw"], writes=["mod"])
        K.op("dve", lambda e: e.tensor_copy(out=modB[:, 3, :], in_=modraw[:, 144:152]), reads=["modraw"], writes=["mod"])
        K.op("dve", lambda e: e.scalar_tensor_tensor(out=modA[:, 3, :], in0=modraw[:, 152:160], scalar=1.0,
                                                     in1=ng_all[:, 48:56], op0=ALU.add, op1=ALU.mult),
             reads=["modraw", "ng_all"], writes=["mod"])
        K.op("dve", lambda e: e.tensor_copy(out=modA[:, 7, :], in_=ng_all[:, 56:64]), reads=["ng_all"], writes=["mod"])
        K.barrier()

    def XK(name, b):
        return [(name, b, c) for c in range(8)]

    def norm_mod(xt, xkeys, T, ni, sq, rs, tmp, h, hname, hb, psb, add_shift=True):
        K.op("act", lambda e: e.activation(out=sq[:, :, :T], in_=xt[:, :, :T], func=AF.Square), reads=xkeys, writes=["sq"])
        for c in range(8):
            K.op("pe", lambda e, c=c: e.matmul(ps[psb][:, :T], lhsT=ones_bf[:], rhs=sq[:, c, :T],
                                                start=(c == 0), stop=(c == 7)),
                 reads=["sq", "ones"], writes=[("ps", psb)], sig=(c == 7))
        K.op("act", lambda e: e.activation(out=rs[:, :T], in_=ps[psb][:, :T], func=AF.Sqrt, bias=EPS, scale=1.0 / D),
             reads=[("ps", psb)], writes=["rs"])
        K.op("dve", lambda e: e.reciprocal(out=rs[:, :T], in_=rs[:, :T]), reads=["rs"], writes=["rs"])
        for c in range(8):
            if not add_shift:
                K.op("dve", lambda e, c=c: e.scalar_tensor_tensor(
                    out=xt[:, c, :T], in0=xt[:, c, :T], scalar=modA[:, ni, c:c + 1], in1=rs[:, :T],
                    op0=ALU.mult, op1=ALU.mult), reads=[xkeys[c], "rs", "mod"], writes=[xkeys[c]])
                continue
            K.op("dve", lambda e, c=c: e.scalar_tensor_tensor(
                out=tmp[:, c, :T], in0=xt[:, c, :T], scalar=modA[:, ni, c:c + 1], in1=rs[:, :T],
                op0=ALU.mult, op1=ALU.mult), reads=[xkeys[c], "rs", "mod"], writes=[("tmp", c)])
            if add_shift:
                K.op("act", lambda e, c=c: e.activation(out=h[:, c, :T], in_=tmp[:, c, :T], func=AF.Identity,
                                                        bias=modB[:, ni, c:c + 1], scale=1.0),
                     reads=[("tmp", c), "mod"], writes=[(hname, hb, c)])

    def load_w(dst, dkey, w, kchunks, per=4):
        w_ap, wkey = w
        wv = w_ap.rearrange("(k p) f -> p k f", p=128)
        for k0 in range(0, kchunks, per):
            k1 = min(kchunks, k0 + per)
            K.dma("pool", dst[:, k0:k1, :], wv[:, k0:k1, :], reads=[wkey] if wkey else [], writes=[dkey])

    def phase_ffn(l, s_, src, dst, pre=None, final=False):
        T = 256
        ntile = NT // T
        ni = l * 4 + (0 if s_ == 0 else 2)
        gi = l * 3 + (0 if s_ == 0 else 2)
        with ExitStack() as st:
            win = sb("win", [128, 8, 2 * FH], BF16, st)
            wout = sb("wout", [128, 22, D], BF16, st)
            xt = [sb("xt%d" % i, [128, 8, T], F32, st) for i in range(2)]
            sq = sb("sq", [128, 8, T], BF16, st)
            rs = sb("rs", [128, T], F32, st)
            tmp = sb("tmp", [128, 8, T], F32, st)
            h = [sb("h%d" % i, [128, 8, T], BF16, st) for i in range(2)]
            sg = [sb("sg%d" % i, [128, T], F32, st) for i in range(2)]
            act = sb("act", [128, 22, T], BF16, st)
            if pre is not None:
                wo = sb("wo", [128, 8, D], BF16, st)
                at1 = sb("at", [128, 8, T], BF16, st)
                at = [at1, at1]
                load_w(wo, "wo", pre[0], 8)
            load_w(win, "win", ffn_w_in[l, s_], 8, per=1)
            load_w(wout, "wout", ffn_w_out[l, s_], 22, per=6)
            srcv = src.rearrange("(c p) t -> p c t", p=128)
            dstv = dst.rearrange("(c p) t -> p c t", p=128)
            if pre is not None:
                atv = attnT.rearrange("(c p) t -> p c t", p=128)

            def load(i):
                b = i % 2
                K.dma("sp", xt[b][:], srcv[:, :, i * T:(i + 1) * T], writes=XK("xt", b))
                if pre is not None:
                    K.dma("sp", at[b][:], atv[:, :, i * T:(i + 1) * T], writes=[("at", 0)])

            def prep(i):
                b = i % 2
                if pre is not None:
                    for dc in range(8):
                        pb = 5 + dc % 2
                        for c in range(8):
                            K.op("pe", lambda e, b=b, dc=dc, c=c, pb=pb: e.matmul(
                                ps[pb][:, :T], lhsT=wo[:, c, dc * 128:(dc + 1) * 128], rhs=at[b][:, c, :],
                                start=(c == 0), stop=(c == 7)),
                                reads=["wo", ("at", 0)], writes=[("ps", pb)], sig=(c == 7))
                        K.op("dve", lambda e, b=b, dc=dc, pb=pb: e.scalar_tensor_tensor(
                            out=xt[b][:, dc, :], in0=ps[pb][:, :T], scalar=modG[:, pre[1], dc:dc + 1],
                            in1=xt[b][:, dc, :], op0=ALU.mult, op1=ALU.add),
                            reads=[("ps", pb), ("xt", b, dc), "mod"], writes=[("xt", b, dc)])
                norm_mod(xt[b], XK("xt", b), T, ni, sq, rs, tmp, h[b], "h", b, 0)

            def pass1(i):
                b = i % 2
                for fc in range(22):
                    pg, pu = 1 + fc % 2, 3 + fc % 2
                    for k in range(8):
                        K.op("pe", lambda e, fc=fc, k=k, pg=pg: e.matmul(
                            ps[pg][:, :T], lhsT=win[:, k, fc * 128:(fc + 1) * 128], rhs=h[b][:, k, :],
                            start=(k == 0), stop=(k == 7)),
                            reads=["win", ("h", b, k)], writes=[("ps", pg)], sig=(k == 7))
                    for k in range(8):
                        K.op("pe", lambda e, fc=fc, k=k, pu=pu: e.matmul(
                            ps[pu][:, :T], lhsT=win[:, k, FH + fc * 128:FH + (fc + 1) * 128], rhs=h[b][:, k, :],
                            start=(k == 0), stop=(k == 7)),
                            reads=["win", ("h", b, k)], writes=[("ps", pu)], sig=(k == 7))
                    K.op("act", lambda e, fc=fc, pg=pg: e.activation(out=sg[fc % 2][:], in_=ps[pg][:, :T], func=AF.Silu),
                         reads=[("ps", pg)], writes=[("sg", fc % 2)])
                    K.op("dve", lambda e, fc=fc, pu=pu: e.tensor_tensor(out=act[:, fc, :], in0=ps[pu][:, :T],
                                                                        in1=sg[fc % 2][:], op=ALU.mult),
                         reads=[("ps", pu), ("sg", fc % 2)], writes=[("act", fc)])

            def pass2(i):
                b = i % 2
                for dc in range(8):
                    pb = 5 + dc % 2
                    for fc in range(22):
                        K.op("pe", lambda e, dc=dc, fc=fc, pb=pb: e.matmul(
                            ps[pb][:, :T], lhsT=wout[:, fc, dc * 128:(dc + 1) * 128], rhs=act[:, fc, :],
                            start=(fc == 0), stop=(fc == 21)),
                            reads=["wout", ("act", fc)], writes=[("ps", pb)], sig=(fc == 21))
                    K.op("dve", lambda e, dc=dc, pb=pb: e.scalar_tensor_tensor(
                        out=xt[b][:, dc, :], in0=ps[pb][:, :T], scalar=modG[:, gi, dc:dc + 1],
                        in1=xt[b][:, dc, :], op0=ALU.mult, op1=ALU.add),
                        reads=[("ps", pb), ("xt", b, dc), "mod"], writes=[("xt", b, dc)])

            def post(i):
                b = i % 2
                if final:
                    norm_mod(xt[b], XK("xt", b), T, 7, sq, rs, tmp, None, None, None, 7, add_shift=False)
                    K.dma("sp", dstv[:, :, i * T:(i + 1) * T], xt[b][:], reads=XK("xt", b))
                else:
                    K.dma("sp", dstv[:, :, i * T:(i + 1) * T], xt[b][:], reads=XK("xt", b))

            load(0)
            prep(0)
            for i in range(ntile):
                if i + 1 < ntile:
                    load(i + 1)
                pass1(i)
                if i + 1 < ntile:
                    prep(i + 1)
                pass2(i)
                post(i)
            K.barrier()

    def phase_proj(src, ni, w_ap, nout, fm_outs, tm=None, fgate=False):
        T = 512
        ntile = NT // T
        with ExitStack() as st:
            w = sb("pw", [128, 8, nout], BF16, st)
            xt = [sb("pxt%d" % i, [128, 8, T], F32, st) for i in range(2)]
            sq = sb("psq", [128, 8, T], BF16, st)
            rs = sb("prs", [128, T], F32, st)
            tmp = sb("ptmp", [128, 8, T], F32, st)
            h = sb("ph", [128, 8, T], BF16, st)
            nfm = sum(n for _, n, _, _ in fm_outs)
            stage = sb("pstage", [128, max(nfm, 1), T], BF16, st)
            if tm is not None:
                hd = tm[1]
                if hd == 128:
                    vst = sb("pvst", [128, 4, D], BF16, st)
                else:
                    vst = sb("pvst", [128, 4, 16, 65], BF16, st)
                    K.op("pool", lambda e: e.memset(vst[:], 1.0), writes=[("vst", a_, b_) for a_ in range(4) for b_ in range(2)])
            if fgate:
                fgw = sb("fgw", [128, 8, 16], BF16, st)
                K.dma("pool", fgw[:], fgate_wT.rearrange("p (k h) -> p k h", h=16), writes=["fgw"])
                e1 = sb("fge1", [16, T], F32, st)
                lgs = sb("fglg", [16, T], F32, st)
            load_w(w, "pw", w_ap, 8, per=2)
            srcv = src.rearrange("(c p) t -> p c t", p=128)

            K.dma("sp", xt[0][:], srcv[:, :, 0:T], writes=XK("pxt", 0))
            for i in range(ntile):
                b = i % 2
                if i + 1 < ntile:
                    K.dma("sp", xt[1 - b][:], srcv[:, :, (i + 1) * T:(i + 2) * T], writes=XK("pxt", 1 - b))
                norm_mod(xt[b], XK("pxt", b), T, ni, sq, rs, tmp, h, "ph", 0, 0)
                hk = [("ph", 0, c) for c in range(8)]
                si = 0
                g = 0
                for (col0, nch, dstd, scale) in fm_outs:
                    for oc in range(nch):
                        pb = 1 + g % 4
                        for k in range(8):
                            K.op("pe", lambda e, k=k, pb=pb, c0=col0 + oc * 128: e.matmul(
                                ps[pb][:, :T], lhsT=w[:, k, c0:c0 + 128], rhs=h[:, k, :],
                                start=(k == 0), stop=(k == 7)),
                                reads=["pw", hk[k]], writes=[("ps", pb)], sig=(k == 7))
                        if g % 2 == 0:
                            K.op("act", lambda e, pb=pb, si=si, scale=scale: e.activation(
                                out=stage[:, si, :], in_=ps[pb][:, :T], func=AF.Copy, scale=scale),
                                reads=[("ps", pb)], writes=[("stage", si)])
                        else:
                            K.op("dve", lambda e, pb=pb, si=si, scale=scale: e.tensor_scalar(
                                out=stage[:, si, :], in0=ps[pb][:, :T], scalar1=scale, scalar2=0.0, op0=ALU.mult, op1=ALU.add),
                                reads=[("ps", pb)], writes=[("stage", si)])
                        si += 1
                        g += 1
                    dv = dstd.rearrange("(c p) t -> p c t", p=128)
                    K.dma("sp", dv[:, :, i * T:(i + 1) * T], stage[:, si - nch:si, :],
                          reads=[("stage", j) for j in range(si - nch, si)])
                if tm is not None:
                    col0, hd, dstd = tm
                    for tb in range(4):
                        for eh in range(2):
                            pb = 1 + g % 4
                            for k in range(8):
                                K.op("pe", lambda e, k=k, pb=pb, tb=tb, c0=col0 + eh * 512: e.matmul(
                                    ps[pb][:, :512], lhsT=h[:, k, tb * 128:(tb + 1) * 128], rhs=w[:, k, c0:c0 + 512],
                                    start=(k == 0), stop=(k == 7)),
                                    reads=["pw", hk[k]], writes=[("ps", pb)], sig=(k == 7))
                            if hd == 128:
                                o_ap = vst[:, tb, eh * 512:(eh + 1) * 512]
                                i_ap = ps[pb][:, :512]
                            else:
                                o_ap = vst[:, tb, eh * 8:(eh + 1) * 8, 0:64]
                                i_ap = ps[pb][:, :512].rearrange("p (h e) -> p h e", e=64)
                            if g % 2 == 0:
                                K.op("act", lambda e, o_ap=o_ap, i_ap=i_ap: e.activation(out=o_ap, in_=i_ap, func=AF.Copy),
                                     reads=[("ps", pb)], writes=[("vst", tb, eh)])
                            else:
                                K.op("dve", lambda e, o_ap=o_ap, i_ap=i_ap: e.tensor_copy(out=o_ap, in_=i_ap),
                                     reads=[("ps", pb)], writes=[("vst", tb, eh)])
                            g += 1
                        blk = i * 4 + tb
                        if hd == 128:
                            dv = dstd.rearrange("(h p) (b e) -> p h b e", p=128, e=128)
                            K.dma("sp", dv[:, :, blk, :], vst[:, tb, :].rearrange("p (h e) -> p h e", e=128), reads=[("vst", tb, 0), ("vst", tb, 1)])
                        else:
                            dv = dstd.rearrange("(h p) (b e) -> p h b e", p=128, e=65)
                            K.dma("sp", dv[:, :, blk, :], vst[:, tb, :, :], reads=[("vst", tb, 0), ("vst", tb, 1)])
                if fgate:
                    for k in range(8):
                        K.op("pe", lambda e, k=k: e.matmul(ps[5][0:16, :T], lhsT=fgw[:, k, :], rhs=h[:, k, :],
                                                            start=(k == 0), stop=(k == 7)),
                             reads=["fgw", hk[k]], writes=[("ps", 5)], sig=(k == 7))
                    K.op("act", lambda e: e.activation(out=e1[:], in_=ps[5][0:16, :T], func=AF.Exp, bias=nfgb[:, 0:1], scale=-1.0),
                         reads=[("ps", 5), "nfgb"], writes=["e1"])
                    K.op("act", lambda e: e.activation(out=e1[:], in_=e1[:], func=AF.Ln, bias=1.0, scale=1.0),
                         reads=["e1"], writes=["e1"])
                    K.op("dve", lambda e: e.tensor_scalar(out=lgs[:], in0=e1[:], scalar1=-1.0, scalar2=0.0, op0=ALU.mult, op1=ALU.add),
                         reads=["e1"], writes=["lgs"])
                    K.dma("sp", lf_loc[:, i * T:(i + 1) * T], lgs[:], reads=["lgs"])
            K.barrier()

    def setup_A(st):
        oh = sb("oh", [128, 2, 32, 128], F32, st)
        ngc = sb("ngc", [128, 128], F32, st)
        Tn = sb("Tn", [128, 32, 8], F32, st)
        DP = sb("DP", [128, 8, 2, 128], F32, st)
        cf = sb("cf", [128, NZ * 4, 3], F32, st)
        K.dma("sp", oh[:], onehot.rearrange("p (a b q) -> p a b q", a=2, b=32), writes=["oh"])
        K.dma("sp", ngc[:], negc, writes=["ngc"])
        K.dma("sp", Tn[:], relb.rearrange("p (b h) -> p b h", h=8), writes=["Tn"])
        K.dma("sp", cf[:], coefA.rearrange("p (z c) -> p z c", c=3), writes=["cf"])
        for bk in range(31):
            K.op("dve", lambda e, bk=bk: e.tensor_tensor(out=Tn[:, bk, :], in0=Tn[:, bk, :], in1=Tn[:, 31, :],
                                                          op=ALU.subtract), reads=["Tn"], writes=["Tn"])
        for hh in range(8):
            K.op("dve", lambda e, hh=hh: e.tensor_copy(out=DP[:, hh, 0, :], in_=ngc[:]), reads=["ngc"], writes=["DP"])
            K.op("dve", lambda e, hh=hh: e.memset(DP[:, hh, 1, :], 0.0), writes=["DP"])
            for a_ in range(2):
                for bk in range(31):
                    K.op("dve", lambda e, hh=hh, a_=a_, bk=bk: e.scalar_tensor_tensor(
                        out=DP[:, hh, a_, :], in0=oh[:, a_, bk, :], scalar=Tn[:, bk, hh:hh + 1],
                        in1=DP[:, hh, a_, :], op0=ALU.mult, op1=ALU.add),
                        reads=["oh", "Tn", "DP"], writes=["DP"])
        return oh, ngc, Tn, DP, cf

    def phase_attn(layer, preA=None):
        isA = layer == 0
        NH = 8 if isA else 16
        KC = 128 if isA else 70
        nmap = 2 if isA else 1
        VW = 128 if isA else 65
        TQ = 512
        ntile = NT // TQ
        with ExitStack() as st:
            Kt = [sb("Kt%d" % i, [KC, NR * NT], BF16, st) for i in range(2)]
            Qt = [sb("Qt%d" % i, [KC, NT], BF16, st) for i in range(2)]
            Vt = [sb("Vt%d" % i, [128, NR * NBLK, VW], BF16, st) for i in range(2)]
            rz = sb("rz", [128, TQ], F32, st)
            ast = [sb("ast%d" % i, [128, TQ], BF16, st) for i in range(2)]
            if isA:
                Mk = [sb("Mk%d" % i, [128, NZ, TQ], BF16, st) for i in range(2)]
                oh, ngc, Tn, DP, cf = preA
                t1 = sb("t1", [128, 128], F32, st)
                oA = sb("oA", [128, TQ], F32, st)
                on = sb("on", [128, TQ], F32, st)
                osq = sb("osq", [128, TQ], BF16, st)
                ors = sb("ors", [128, TQ], F32, st)
            else:
                Mk1 = sb("MkB", [128, NZ, TQ], BF16, st)
                Mk = [Mk1, Mk1]
                K.dma("pool", Mk1[:], maskB.rearrange("p (z q) -> p z q", q=TQ), writes=[("Mk", 0), ("Mk", 1)])
                bc = sb("bcB", [64, TQ], F32, st)
                for i in range(2):
                    K.op("pool", lambda e, i=i: e.memset(Kt[i][64:70, :], 1.0), writes=[("Kt", i)])
                    K.op("pool", lambda e, i=i: e.memset(Qt[i][64:70, :], 1.0), writes=[("Qt", i)])
                    K.op("pool", lambda e, i=i: e.memset(Vt[i][:, :, 64:65], 1.0), writes=[("Vt", i)])

            def load_head(hh):
                b = hh % 2
                if isA:
                    for r in range(NR):
                        for hf in range(2):
                            ro = ((hh * 2 + hf) * NR + r) * 64
                            K.dma("sp", Kt[b][hf * 64:(hf + 1) * 64, r * NT:(r + 1) * NT], kT_all[ro:ro + 64, :],
                                  writes=[("Kt", b)])
                            K.dma("sp", Vt[b][hf * 64:(hf + 1) * 64, r * NBLK:(r + 1) * NBLK, :],
                                  v_all[ro:ro + 64, :].rearrange("p (b e) -> p b e", e=128),
                                  writes=[("Vt", b)])
                    K.dma("sp", Qt[b][:], qT[hh * 128:(hh + 1) * 128, :], writes=[("Qt", b)])
                    for z in range(NZ):
                        for c in range(4):
                            zi = z * 4 + c
                            K.op("dve", lambda e, hh=hh, zi=zi: e.tensor_scalar(
                                out=t1[:], in0=DP[:, hh, 1, :], scalar1=cf[:, zi, 1:2], scalar2=cf[:, zi, 2:3],
                                op0=ALU.mult, op1=ALU.add), reads=["DP", "cf"], writes=["t1"])
                            K.op("dve", lambda e, hh=hh, zi=zi, z=z, c=c, b=b: e.scalar_tensor_tensor(
                                out=Mk[b][:, z, c * 128:(c + 1) * 128], in0=DP[:, hh, 0, :], scalar=cf[:, zi, 0:1],
                                in1=t1[:], op0=ALU.mult, op1=ALU.add), reads=["DP", "cf", "t1"], writes=[("Mk", b)])
                else:
                    for r in range(NR):
                        ro = (hh * NR + r) * 64
                        K.dma("sp", Kt[b][0:64, r * NT:(r + 1) * NT], kT_all[ro:ro + 64, :],
                              writes=[("Kt", b)])
                        K.dma("sp", Vt[b][:, r * NBLK:(r + 1) * NBLK, :],
                              vb_all[(hh * NR + r) * 128:(hh * NR + r + 1) * 128, :].rearrange("p (b e) -> p b e", e=65),
                              writes=[("Vt", b)])
                    K.dma("sp", Kt[b][67:70, :], FkT[hh * 3:(hh + 1) * 3, :], writes=[("Kt", b)])
                    K.dma("sp", Qt[b][0:64, :], qT[hh * 64:(hh + 1) * 64, :], writes=[("Qt", b)])
                    K.dma("sp", Qt[b][64:67, :], FqT[hh * 3:(hh + 1) * 3, :], writes=[("Qt", b)])

            it = [0]
            NPT = 4
            LOOKG = 2
            pt2 = [sb("pt2_%d" % i, [128, 2 * TQ], BF16, st) for i in range(NPT)]

            def attend(hh, j, m):
                b = hh % 2
                a_i = it[0] % 2
                it[0] += 1
                if isA:
                    po, pz = 6, 7
                else:
                    po, pz = 6 + a_i, None
                nz, zn = [], []
                for r in range(NR):
                    for i in range(4 * j + 4):
                        if i >= 4 * j:
                            zn.append((r * NBLK + i, r * 4 + (i - 4 * j)))
                        elif isA and r == NR - 1 and i == 4 * j - 1:
                            zn.append((r * NBLK + i, NZ - 1))
                        else:
                            nz.append((r * NBLK + i, None))
                groups = [nz[x:x + 2] for x in range(0, len(nz), 2)] + [zn[x:x + 2] for x in range(0, len(zn), 2)]
                ng = len(groups)
                nkb = len(nz) + len(zn)
                if isA:
                    k0, k1 = m * 64, m * 64 + 64
                else:
                    k0, k1 = 0, 70
                cnt = [0]
                for gi in range(ng + LOOKG):
                    if gi < ng:
                        grp = groups[gi]
                        s2 = gi % 3
                        W = TQ * len(grp)
                        for e_, (kb, z) in enumerate(grp):
                            bk = 2 * s2 + e_
                            zone = z is not None
                            K.op("pe", lambda e, kb=kb, bk=bk, zone=zone: e.matmul(
                                ps[bk][:, :TQ], lhsT=Kt[b][k0:k1, kb * 128:(kb + 1) * 128],
                                rhs=Qt[b][k0:k1, j * TQ:(j + 1) * TQ], start=True, stop=not zone),
                                reads=[("Kt", b), ("Qt", b)], writes=[("ps", bk)], sig=(e_ == len(grp) - 1) and not zone)
                            if zone:
                                K.op("pe", lambda e, bk=bk, z=z: e.matmul(
                                    ps[bk][:, :TQ], lhsT=ident_bf[:], rhs=Mk[b][:, z, :], start=False, stop=True),
                                    reads=["ident", ("Mk", b)], writes=[("ps", bk)], sig=(e_ == len(grp) - 1))
                        bks = [("ps", 2 * s2 + e_) for e_ in range(len(grp))]
                        if True:
                            K.op("act", lambda e, gi=gi, s2=s2, W=W: e.activation(
                                out=pt2[gi % NPT][:, :W], in_=psbig[:, 2 * s2 * TQ:2 * s2 * TQ + W], func=AF.Exp),
                                reads=bks, writes=[("pt2", gi % NPT)])
                    if gi >= LOOKG:
                        gp = gi - LOOKG
                        grp = groups[gp]
                        for e_, (kb, z) in enumerate(grp):
                            u = cnt[0]
                            cnt[0] += 1
                            K.op("pe", lambda e, kb=kb, u=u, e_=e_, gp=gp: e.matmul(
                                ps[po][0:VW, :TQ], lhsT=Vt[b][:, kb, :], rhs=pt2[gp % NPT][:, e_ * TQ:(e_ + 1) * TQ],
                                start=(u == 0), stop=(u == nkb - 1)),
                                reads=[("Vt", b), ("pt2", gp % NPT)], writes=[("ps", po)],
                                sig=(not isA) and e_ == len(grp) - 1)
                            if isA:
                                K.op("pe", lambda e, u=u, e_=e_, gp=gp: e.matmul(
                                    ps[pz][:, :TQ], lhsT=ones_bf[:], rhs=pt2[gp % NPT][:, e_ * TQ:(e_ + 1) * TQ],
                                    start=(u == 0), stop=(u == nkb - 1)),
                                    reads=["ones", ("pt2", gp % NPT)], writes=[("ps", pz)], sig=(e_ == len(grp) - 1))
                return po, pz

            def epilogue_A(hh, j, m, po, pz, scr):
                K.op("act", lambda e: e.activation(out=rz[:], in_=ps[pz][:, :TQ], func=AF.Ln), reads=[("ps", pz)], writes=["rz"])
                K.op("act", lambda e: e.activation(out=rz[:], in_=rz[:], func=AF.Exp, scale=-1.0), reads=["rz"], writes=["rz"])
                if m == 0:
                    K.op("dve", lambda e: e.tensor_tensor(out=oA[:], in0=ps[po][:, :TQ], in1=rz[:], op=ALU.mult),
                         reads=[("ps", po), "rz"], writes=["oA"])
                    return
                K.op("dve", lambda e: e.tensor_tensor(out=on[:], in0=ps[po][:, :TQ], in1=rz[:], op=ALU.mult),
                     reads=[("ps", po), "rz"], writes=["on"])
                K.op("dve", lambda e: e.scalar_tensor_tensor(out=on[:], in0=on[:], scalar=nlam[:, 0:1], in1=oA[:],
                                                             op0=ALU.mult, op1=ALU.add),
                     reads=["on", "oA", "nlam"], writes=["on"])
                K.op("act", lambda e: e.activation(out=osq[:], in_=on[:], func=AF.Square), reads=["on"], writes=["osq"])
                K.op("pe", lambda e: e.matmul(ps[scr][:, :TQ], lhsT=ones_bf[:], rhs=osq[:], start=True, stop=True),
                     reads=["ones", "osq"], writes=[("ps", scr)])
                K.op("act", lambda e: e.activation(out=ors[:], in_=ps[scr][:, :TQ], func=AF.Ln, bias=EPS, scale=1.0 / 128),
                     reads=[("ps", scr)], writes=["ors"])
                K.op("act", lambda e: e.activation(out=ors[:], in_=ors[:], func=AF.Exp, scale=-0.5), reads=["ors"], writes=["ors"])
                sb_ = j % 2
                K.op("dve", lambda e: e.scalar_tensor_tensor(out=ast[sb_][:], in0=on[:], scalar=gsub[:, 0:1], in1=ors[:],
                                                             op0=ALU.mult, op1=ALU.mult),
                     reads=["on", "ors", "gsub"], writes=[("ast", sb_)])
                K.dma("sp", attnT[hh * 128:(hh + 1) * 128, j * TQ:(j + 1) * TQ], ast[sb_][:], reads=[("ast", sb_)])

            def epilogue_B(hh, j, po):
                K.op("act", lambda e: e.activation(out=rz[64:65, :], in_=ps[po][64:65, :TQ], func=AF.Ln), reads=[("ps", po)], writes=["rz"])
                K.op("act", lambda e: e.activation(out=rz[64:65, :], in_=rz[64:65, :], func=AF.Exp, scale=-1.0), reads=["rz"], writes=["rz"])
                K.op("pe", lambda e: e.matmul(ps[0][0:64, :TQ], lhsT=ones_f[64:65, 0:64], rhs=rz[64:65, :], start=True, stop=True),
                     reads=["onesf", "rz"], writes=[("ps", 0)])
                K.op("act", lambda e: e.activation(out=bc[:], in_=ps[0][0:64, :TQ], func=AF.Copy), reads=[("ps", 0)], writes=["bc"])
                sb_ = j % 2
                K.op("dve", lambda e: e.tensor_tensor(out=ast[sb_][0:64, :], in0=ps[po][0:64, :TQ], in1=bc[:], op=ALU.mult),
                     reads=[("ps", po), "bc"], writes=[("ast", sb_)])
                K.dma("sp", attnT[hh * 64:(hh + 1) * 64, j * TQ:(j + 1) * TQ], ast[sb_][0:64, :], reads=[("ast", sb_)])

            load_head(0)
            for hh in range(NH):
                if hh + 1 < NH:
                    load_head(hh + 1)
                for j in range(ntile):
                    prev_pz = None
                    for m in range(nmap):
                        po, pz = attend(hh, j, m)
                        if isA:
                            epilogue_A(hh, j, m, po, pz, 0)
                            prev_pz = pz
                        else:
                            epilogue_B(hh, j, po)
            K.barrier()

    def phase_F():
        with ExitStack() as st:
            bufA = sb("fA", [16, S], F32, st)
            bufB = sb("fB", [16, S], F32, st)
            Fg = sb("fFg", [16, S], F32, st)
            FkS = sb("fFkS", [16, 3, NR * NT], BF16, st)
            Fq = sb("fFq", [16, NT], F32, st)
            FqS = sb("fFqS", [16, 3, NT], BF16, st)
            Lv = bufA[:].rearrange("h (i r p) -> h i r p", r=NR, p=128)
            for r in range(NR):
                K.dma("sp", Lv[:, :, r, :], lf_all[r * 16:(r + 1) * 16, :].rearrange("h (i p) -> h i p", p=128),
                      writes=["fA"])
            K.op("dve", lambda e: e.memset(bufB[:], 1.0), writes=["fB"])
            K.op("dve", lambda e: e.tensor_tensor_scan(out=Fg[:], data0=bufB[:], data1=bufA[:], initial=0.0,
                                                       op0=ALU.mult, op1=ALU.add),
                 reads=["fA", "fB"], writes=["Fg"])
            Fv = Fg[:].rearrange("h (i r p) -> h i r p", r=NR, p=128)

            def split3(src_ap, dst3, shp, r1, r1key, negate):
                sgn = -1.0 if negate else 1.0
                K.op("dve", lambda e: e.tensor_scalar(out=r1, in0=src_ap, scalar1=sgn, scalar2=0.0, op0=ALU.mult, op1=ALU.add),
                     reads=["Fg", "Fq"], writes=[r1key])
                for t in range(3):
                    K.op("dve", lambda e, t=t: e.tensor_copy(out=dst3(t), in_=r1), reads=[r1key], writes=["split"])
                    if t < 2:
                        K.op("dve", lambda e, t=t: e.tensor_tensor(out=r1, in0=r1, in1=dst3(t), op=ALU.subtract),
                             reads=[r1key, "split"], writes=[r1key])

            for r in range(NR):
                r1 = bufB[:, 0:NT].rearrange("h (i p) -> h i p", p=128)
                split3(Fv[:, :, r, :], lambda t, r=r: FkS[:, t, r * NT:(r + 1) * NT].rearrange("h (i p) -> h i p", p=128),
                       None, r1, "fB", True)
            if NR == 2:
                Fq3 = Fq[:].rearrange("h (i p) -> h i p", p=128)
                K.op("dve", lambda e: e.tensor_scalar(out=Fq3, in0=Fv[:, :, 0, :], scalar1=rsel_sb[0:16, 0:1], scalar2=0.0,
                                                      op0=ALU.mult, op1=ALU.add), reads=["Fg", "rsel"], writes=["Fq"])
                K.op("dve", lambda e: e.scalar_tensor_tensor(out=Fq3, in0=Fv[:, :, 1, :], scalar=rsel_sb[0:16, 1:2], in1=Fq3,
                                                             op0=ALU.mult, op1=ALU.add),
                     reads=["Fg", "rsel", "Fq"], writes=["Fq"])
                fq_src = Fq[:]
            else:
                fq_src = Fg[:]
            split3(fq_src, lambda t: FqS[:, t, :], None, bufA[:, 0:NT], "fA", False)
            K.dma("sp", FkT.rearrange("(h t) k -> h t k", t=3), FkS[:], reads=["split"])
            K.dma("sp", FqT.rearrange("(h t) k -> h t k", t=3), FqS[:], reads=["split"])
            K.barrier()

    def done(name):
        return stop_after == name

    def finish():
        K.barrier(cc=True)
        for dst_, src_ in dbg_copies:
            if any(src_ is w_ for w_ in coll_written):
                K.dma("sp", dst_, src_)
        K.barrier(cc=True)
        es.close()
        return nc

    phase_ffn(0, 0, xT_in, x_s)
    if done("ffn00"):
        return finish()
    phase_proj(x_s, 1, (a_w_qkv, a_w_qkvk), 3 * D, [(0, 8, qT, 0.125), (D, 8, kT_loc, 1.0)], tm=(2 * D, 128, v_loc))
    K.collective_rows(kT_loc, kT_all, 64)
    K.collective_rows(v_loc, v_all, 64)
    coll_written.extend([kT_loc, kT_all, v_loc, v_all])
    stA = ExitStack()
    preA = setup_A(stA)
    K.barrier(cc=True)
    if done("projA"):
        stA.close()
        return finish()
    phase_attn(0, preA)
    stA.close()
    if done("attnA"):
        return finish()
    phase_ffn(0, 1, x_s, x_s, pre=((a_w_o, a_w_ok), 1))
    if done("ffn01"):
        return finish()
    phase_proj(x_s, 3, (kv_w, kv_wk), 2 * D, [(0, 8, kT_loc, 1.0)], tm=(D, 64, vb_loc), fgate=True)
    K.collective_rows(kT_loc, kT_all, 64)
    K.collective_rows(vb_loc, vb_all, 128)
    K.collective_rows(lf_loc, lf_all, 16)
    coll_written.extend([vb_loc, vb_all, lf_loc, lf_all])
    K.barrier(cc=True)
    phase_F()
    if done("projKV"):
        return finish()
    phase_ffn(1, 0, x_s, x_s)
    phase_proj(x_s, 5, (b_w_q, b_w_qk), D, [(0, 8, qT, 0.125)])
    if done("projQ"):
        return finish()
    phase_attn(1)
    if done("attnB"):
        return finish()
    phase_ffn(1, 1, x_s, outT, pre=((b_w_o, b_w_ok), 4), final=True)
    return finish()


def _t5_bucket(n):
    n = np.maximum(n, 0)
    nf = np.maximum(n, 1).astype(np.float32)
    large = 16 + (np.log(nf / 16) / math.log(128 / 16) * 16).astype(np.int32)
    large = np.minimum(large, 31)
    return np.where(n < 16, n, large)


def _fm(v):
    return np.ascontiguousarray(v.reshape(-1, 128).T)


def make_in_maps(inp):
    f32 = np.float32
    x = np.asarray(inp["x"], f32)
    k = np.arange(128)[:, None]
    q = np.arange(128)[None, :]
    oh = np.zeros((128, 2, 32, 128), f32)
    relD = q - k
    bD = _t5_bucket(relD)
    relP = 128 + q - k
    bP = _t5_bucket(relP)
    for b in range(32):
        oh[:, 0, b, :] = ((bD == b) & (relD >= 0))
        oh[:, 1, b, :] = (bP == b)
    negc = np.where(relD < 0, NEG, 0.0).astype(f32)
    big = {
        "ada_w": np.asarray(inp["ada_w"], f32).reshape(2 * D, 9 * D),
        "kv_ada_w": np.asarray(inp["kv_ada_w"], f32),
        "a_w_qkv": np.asarray(inp["a_w_qkv"][0], f32),
        "a_w_o": np.asarray(inp["a_w_o"][0], f32),
        "kv_w": np.asarray(inp["kv_w"], f32),
        "b_w_q": np.asarray(inp["b_w_q"][0], f32),
        "b_w_o": np.asarray(inp["b_w_o"][0], f32),
    }
    for l_ in range(2):
        for s_ in range(2):
            big["ffn_w_in_%d%d" % (l_, s_)] = np.asarray(inp["ffn_w_in"][l_, s_], f32)
            big["ffn_w_out_%d%d" % (l_, s_)] = np.asarray(inp["ffn_w_out"][l_, s_], f32)
    shared = {
        "ada_bT": np.concatenate([_fm(np.asarray(inp["ada_b"][l], f32)) for l in range(2)], 1),
        "norm_gT": np.concatenate([_fm(np.asarray(inp["norm_g"][l, s], f32)) for l in range(2) for s in range(3)], 1),
        "a_lam": np.ascontiguousarray(np.asarray(inp["a_lambda"][0], f32).reshape(1, 256)),
        "a_subg": np.ascontiguousarray(np.asarray(inp["a_subln_g"][0], f32).reshape(128, 1)),
        "relb": np.ascontiguousarray(np.broadcast_to(np.asarray(inp["rel_bias"], f32).reshape(1, 256), (128, 256))),
        "kv_ada_bT": _fm(np.asarray(inp["kv_ada_b"], f32)),
        "kv_norm_gT": _fm(np.asarray(inp["kv_norm_g"], f32)),
        "fgate_wT": np.ascontiguousarray(np.asarray(inp["fgate_w"], f32).reshape(8, 128, 16).transpose(1, 0, 2).reshape(128, 128)),
        "fgate_bT": np.ascontiguousarray(np.asarray(inp["fgate_b"], f32).reshape(16, 1)),
        "final_gT": _fm(np.asarray(inp["final_g"], f32)),
        "onehot": oh.reshape(128, -1),
        "negc": negc,
        "ident": np.eye(128, dtype=f32),
    }
    per_rank = []
    for r in range(NR):
        coef = np.zeros((128, NZ, 4, 3), f32)
        mB = np.zeros((128, NZ, 4, 128), f32)
        for rp, a in [(rp_, a_) for rp_ in range(NR) for a_ in range(4)] + [(NR - 1, -1)]:
            if True:
                z = rp * 4 + a if a >= 0 else NZ - 1
                for c in range(4):
                    diff = NR * (a - c) + (rp - r)
                    if diff > 0:
                        coef[:, z, c, 2] = NEG
                        mB[:, z, c, :] = NEG
                    elif diff == 0:
                        coef[:, z, c, 0] = 1.0
                        mB[:, z, c, :] = negc
                    elif diff == -1:
                        coef[:, z, c, 1] = 1.0
        rs = np.zeros((128, 2), f32)
        rs[:, 0] = 1.0 - r
        rs[:, 1] = r
        per_rank.append({"coefA": coef.reshape(128, -1), "maskB": mB.reshape(128, -1), "rsel": rs})
    maps = []
    for core in range(NCORE):
        b, r = core // NR, core % NR
        xb = x[b].reshape(NBLK, NR, 128, D)[:, r].reshape(NT, D)
        m = dict(shared)
        m.update(per_rank[r])
        for wn, wa in big.items():
            if WSHARD:
                rows = wa.shape[0] // NCORE
                m[wn] = np.ascontiguousarray(wa[core * rows:(core + 1) * rows])
            else:
                m[wn] = np.ascontiguousarray(wa)
        m["xT"] = np.ascontiguousarray(xb.T)
        m["cT"] = _fm(np.asarray(inp["c"][b], f32))
        maps.append(m)
    return maps


def assemble(outs):
    y = np.zeros((NB, S, D), np.float32)
    for core in range(NCORE):
        b, r = core // NR, core % NR
        o = np.asarray(outs[core]).T.reshape(NBLK, 128, D)
        y[b].reshape(NBLK, NR, 128, D)[:, r] = o
    return y


def kernel(**inputs):
    nc = build()
    in_maps = make_in_maps(inputs)
    res = run_bass_kernel_spmd(nc, in_maps, core_ids=list(range(NCORE)))
    return assemble([res.results[i]["outT"] for i in range(NCORE)])
```

```python
import math
from contextlib import ExitStack

import numpy as np
import concourse.bass as bass
import concourse.mybir as mybir
from concourse.bass_utils import run_bass_kernel_spmd

F32 = mybir.dt.float32
BF16 = mybir.dt.bfloat16
AF = mybir.ActivationFunctionType
ALU = mybir.AluOpType

D = 1024
S = 8192
NB = 4
NR = 2
NT = S // NR
NCORE = 8
FH = 2816
EPS = 1e-6
NEG = -1.0e4
NBLK = NT // 128
NZ = NR * 4 + 1
SAME_SYNC = True
WSHARD = False


class KB:
    def __init__(self, nc, es):
        self.nc = nc
        self.E = {"pe": nc.tensor, "act": nc.scalar, "dve": nc.vector, "pool": nc.gpsimd, "sp": nc.sync}
        self.sem = {}
        for e in self.E:
            self.sem[e] = es.enter_context(nc.semaphore("s_" + e))
        self.sem["cc"] = es.enter_context(nc.semaphore("s_cc"))
        self.cnt = {k: 0 for k in self.sem}
        self.KD = 6
        self.dq = {}
        for q in ("sp", "pool"):
            for i in range(self.KD):
                k = ("d", q, i)
                self.sem[k] = es.enter_context(nc.semaphore("d_%s%d" % (q, i)))
                self.cnt[k] = 0
            self.dq[q] = 0
        self.waited = {e: {} for e in self.E}
        self.lastw = {}
        self.readers = {}
        self.nins = 0

    def _wait(self, eng, s, v):
        if v <= 0:
            return
        if s == eng and (eng == "pe" or not SAME_SYNC):
            return
        if self.waited[eng].get(s, 0) >= v:
            return
        self.E[eng].wait_ge(self.sem[s], v)
        self.waited[eng][s] = v

    def _deps(self, eng, reads, writes):
        for k in reads:
            t = self.lastw.get(k)
            if t is not None:
                self._wait(eng, t[0], t[1])
        for k in writes:
            t = self.lastw.get(k)
            if t is not None:
                self._wait(eng, t[0], t[1])
            for s, v in self.readers.get(k, {}).items():
                self._wait(eng, s, v)

    def _record(self, tok, reads, writes):
        for k in reads:
            d = self.readers.setdefault(k, {})
            if d.get(tok[0], 0) < tok[1]:
                d[tok[0]] = tok[1]
        for k in writes:
            self.lastw[k] = tok
            self.readers[k] = {}

    def op(self, eng, fn, reads=(), writes=(), sig=True):
        self._deps(eng, reads, writes)
        ins = fn(self.E[eng])
        self.nins += 1
        if sig:
            self.cnt[eng] += 1
            ins.then_inc(self.sem[eng], 1)
            tok = (eng, self.cnt[eng])
        else:
            tok = (eng, self.cnt[eng] + 1)
        self._record(tok, reads, writes)

    def dma(self, q, out, in_, reads=(), writes=()):
        i = self.dq[q]
        self.dq[q] = (i + 1) % self.KD
        k = ("d", q, i)
        self._wait(q, k, self.cnt[k])
        self._deps(q, reads, writes)
        self.E[q].dma_start(out=out, in_=in_).then_inc(self.sem[k], 16)
        self.nins += 1
        self.cnt[k] += 16
        self._record((k, self.cnt[k]), reads, writes)

    def collective_rows(self, loc, allt, rows_per):
        R = loc.shape[0]
        for c0 in range(0, R, rows_per):
            ci = c0 // rows_per
            self.collective(loc[c0:c0 + rows_per, :], allt[ci * NR * rows_per:(ci + 1) * NR * rows_per, :], [], [])

    def collective(self, ins_ap, outs_ap, reads, writes, allcores=False):
        self._deps("pool", reads, writes)
        if allcores:
            groups = [list(range(NCORE))]
        else:
            groups = [[b * NR + r for r in range(NR)] for b in range(NCORE // NR)]
        self.nc.gpsimd.collective_compute("AllGather", ALU.bypass, replica_groups=groups,
                                          ins=[ins_ap], outs=[outs_ap]).then_inc(self.sem["cc"], 1)
        self.cnt["cc"] += 1
        self._record(("cc", self.cnt["cc"]), reads, writes)

    def barrier(self, cc=False):
        for e in self.E:
            for s in self.sem:
                if s != e and (cc or s != "cc"):
                    self._wait(e, s, self.cnt[s])
        self.lastw = {k: v for k, v in self.lastw.items() if isinstance(k, tuple) and k[0] == "W"}
        self.readers = {}


def build(stop_after=None, debug=False):
    nc = bass.Bass("TRN2", target_bir_lowering=False)
    dbg_kind = "ExternalOutput" if debug else "Internal"

    def din(name, shape, dt=F32):
        return nc.dram_tensor(name, list(shape), dt, kind="ExternalInput").ap()

    dbg_copies = []
    coll_written = []

    def dscr(name, shape, dt=F32, out=False, coll=False):
        if coll:
            t = nc.dram_tensor(name + "_i", list(shape), dt).ap()
            if debug:
                dbg_copies.append((nc.dram_tensor(name, list(shape), dt, kind="ExternalOutput").ap(), t))
            return t
        if out or debug:
            return nc.dram_tensor(name, list(shape), dt, kind="ExternalOutput").ap()
        return nc.dram_tensor(name, list(shape), dt).ap()

    wgather = []

    def dweight(name, R, C):
        if not WSHARD:
            return din(name, [R, C]), None
        sh = din(name, [R // NCORE, C])
        shi = nc.dram_tensor(name + "_shi", [R // NCORE, C], F32).ap()
        full = nc.dram_tensor(name + "_full", [R, C], F32).ap()
        wgather.append((name, sh, shi, full))
        return full, ("W", name)

    xT_in = din("xT", [D, NT])
    cT = din("cT", [128, 8])
    ada_w2, ada_wk = dweight("ada_w", 2 * D, 9 * D)
    ada_w = ada_w2.rearrange("(l d) f -> l d f", l=2)
    ada_bT = din("ada_bT", [128, 144])
    norm_gT = din("norm_gT", [128, 48])
    kv_ada_w, kv_ada_wk = dweight("kv_ada_w", D, 2 * D)
    ffn_w_in = {}
    ffn_w_out = {}
    for l_ in range(2):
        for s_2 in range(2):
            if (l_, s_2) == (0, 1):
                a_w_qkv, a_w_qkvk = dweight("a_w_qkv", D, 3 * D)
                a_w_o, a_w_ok = dweight("a_w_o", D, D)
            if (l_, s_2) == (1, 0):
                kv_w, kv_wk = dweight("kv_w", D, 2 * D)
            if (l_, s_2) == (1, 1):
                b_w_q, b_w_qk = dweight("b_w_q", D, D)
                b_w_o, b_w_ok = dweight("b_w_o", D, D)
            ffn_w_in[l_, s_2] = dweight("ffn_w_in_%d%d" % (l_, s_2), D, 2 * FH)
            ffn_w_out[l_, s_2] = dweight("ffn_w_out_%d%d" % (l_, s_2), FH, D)
    a_lam = din("a_lam", [1, 256])
    a_subg = din("a_subg", [128, 1])
    relb = din("relb", [128, 256])
    kv_ada_bT = din("kv_ada_bT", [128, 16])
    kv_norm_gT = din("kv_norm_gT", [128, 8])
    fgate_wT = din("fgate_wT", [128, 128])
    fgate_bT = din("fgate_bT", [16, 1])
    final_gT = din("final_gT", [128, 8])
    onehot = din("onehot", [128, 2 * 32 * 128])
    negc = din("negc", [128, 128])
    coefA = din("coefA", [128, NZ * 4 * 3])
    maskB = din("maskB", [128, NZ * 512])
    rsel = din("rsel", [128, 2])
    ident_in = din("ident", [128, 128])

    outT = dscr("outT", [D, NT], out=True)
    x_s = dscr("x_s", [D, NT])
    qT = dscr("qT", [D, NT], BF16)
    kT_loc = dscr("kT_loc", [D, NT], BF16, coll=True)
    kT_all = dscr("kT_all", [NR * D, NT], BF16, coll=True)
    v_loc = dscr("v_loc", [D, NT], BF16, coll=True)
    v_all = dscr("v_all", [NR * D, NT], BF16, coll=True)
    attnT = dscr("attnT", [D, NT], BF16)
    vb_loc = dscr("vb_loc", [16 * 128, NBLK * 65], BF16, coll=True)
    vb_all = dscr("vb_all", [NR * 16 * 128, NBLK * 65], BF16, coll=True)
    lf_loc = dscr("lf_loc", [16, NT], coll=True)
    lf_all = dscr("lf_all", [NR * 16, NT], coll=True)
    FkT = dscr("FkT", [16 * 3, NR * NT], BF16)
    FqT = dscr("FqT", [16 * 3, NT], BF16)

    es = ExitStack()
    es.enter_context(nc.allow_low_precision("bf16 matmul operands, fp32 accumulation"))
    K = KB(nc, es)

    uid = [0]

    def sb(name, shape, dt, stack=es):
        uid[0] += 1
        return stack.enter_context(nc.sbuf_tensor("%s_%d" % (name, uid[0]), list(shape), dt))

    psbig = es.enter_context(nc.psum_tensor("psbig", [128, 4096], F32))
    ps = [psbig[:, i * 512:(i + 1) * 512] for i in range(8)]

    for (wname, sh, shi, full) in wgather:
        K.dma("sp", shi, sh, writes=[("Wshi", wname)])
    for (wname, sh, shi, full) in wgather:
        K.collective(shi, full, reads=[("Wshi", wname)], writes=[("W", wname)], allcores=True)

    ones_bf = sb("ones_bf", [128, 128], BF16)
    ones_f = sb("ones_f", [128, 128], F32)
    modA = sb("modA", [128, 8, 8], F32)
    modB = sb("modB", [128, 8, 8], F32)
    modG = sb("modG", [128, 6, 8], F32)
    cact = sb("cact", [128, 8], F32)
    modraw = sb("modraw", [128, 160], F32)
    bias_all = sb("bias_all", [128, 160], F32)
    ng_all = sb("ng_all", [128, 64], F32)
    lam_sb = sb("lam_sb", [1, 256], F32)
    lam_t = sb("lam_t", [1, 8], F32)
    nlam = sb("nlam", [128, 1], F32)
    gsub = sb("gsub", [128, 1], F32)
    nfgb = sb("nfgb", [16, 1], F32)
    rsel_sb = sb("rsel_sb", [128, 2], F32)
    ident_bf = sb("ident_bf", [128, 128], BF16)
    K.dma("pool", ident_bf[:], ident_in, writes=["ident"])

    K.op("dve", lambda e: e.memset(ones_bf[:], 1.0), writes=["ones"])
    K.op("dve", lambda e: e.memset(ones_f[:], 1.0), writes=["onesf"])
    K.dma("sp", cact[:], cT, writes=["cact"])
    K.dma("sp", bias_all[:, 0:144], ada_bT, writes=["bias_all"])
    K.dma("sp", bias_all[:, 144:160], kv_ada_bT, writes=["bias_all"])
    K.dma("sp", ng_all[:, 0:48], norm_gT, writes=["ng_all"])
    K.dma("sp", ng_all[:, 48:56], kv_norm_gT, writes=["ng_all"])
    K.dma("sp", ng_all[:, 56:64], final_gT, writes=["ng_all"])
    K.dma("sp", lam_sb[:], a_lam, writes=["lam"])
    K.dma("sp", gsub[:], a_subg, writes=["gsub"])
    K.dma("sp", nfgb[:], fgate_bT, writes=["nfgb"])
    K.dma("sp", rsel_sb[:], rsel, writes=["rsel"])
    K.op("act", lambda e: e.activation(out=cact[:], in_=cact[:], func=AF.Silu), reads=["cact"], writes=["cact"])
    K.op("dve", lambda e: e.tensor_scalar(out=nfgb[:], in0=nfgb[:], scalar1=-1.0, scalar2=0.0, op0=ALU.mult, op1=ALU.add),
         reads=["nfgb"], writes=["nfgb"])
    linit = 0.8 - 0.6 * math.exp(-0.3 * 0)
    K.op("dve", lambda e: e.tensor_scalar(out=gsub[:], in0=gsub[:], scalar1=1.0 - linit, scalar2=0.0, op0=ALU.mult, op1=ALU.add),
         reads=["gsub"], writes=["gsub"])
    K.op("dve", lambda e: e.tensor_tensor(out=lam_sb[:, 0:64], in0=lam_sb[:, 0:64], in1=lam_sb[:, 64:128], op=ALU.mult),
         reads=["lam"], writes=["lam"])
    K.op("dve", lambda e: e.tensor_tensor(out=lam_sb[:, 128:192], in0=lam_sb[:, 128:192], in1=lam_sb[:, 192:256], op=ALU.mult),
         reads=["lam"], writes=["lam"])
    K.op("dve", lambda e: e.reduce_sum(out=lam_t[:, 0:1], in_=lam_sb[:, 0:64], axis=mybir.AxisListType.X),
         reads=["lam"], writes=["lamt"])
    K.op("dve", lambda e: e.reduce_sum(out=lam_t[:, 1:2], in_=lam_sb[:, 128:192], axis=mybir.AxisListType.X),
         reads=["lam"], writes=["lamt"])
    K.op("act", lambda e: e.activation(out=lam_t[:, 2:4], in_=lam_t[:, 0:2], func=AF.Exp), reads=["lamt"], writes=["lamt"])
    K.op("dve", lambda e: e.tensor_tensor(out=lam_t[:, 4:5], in0=lam_t[:, 3:4], in1=lam_t[:, 2:3], op=ALU.subtract),
         reads=["lamt"], writes=["lamt"])
    K.op("dve", lambda e: e.tensor_scalar(out=lam_t[:, 5:6], in0=lam_t[:, 4:5], scalar1=-linit, scalar2=0.0, op0=ALU.add, op1=ALU.add),
         reads=["lamt"], writes=["lamt"])
    K.op("pe", lambda e: e.matmul(ps[7][:, 0:1], lhsT=ones_f[0:1, :], rhs=lam_t[0:1, 5:6], start=True, stop=True),
         reads=["lamt", "onesf"], writes=[("ps", 7)])
    K.op("dve", lambda e: e.tensor_copy(out=nlam[:], in_=ps[7][:, 0:1]), reads=[("ps", 7)], writes=["nlam"])

    with ExitStack() as st:
        wb = [sb("modw%d" % i, [128, 8, 1152], F32, st) for i in range(2)]
        jobs = []
        for l in range(2):
            for fb in range(8):
                jobs.append((ada_w[l], fb * 1152, 1152, l * 72 + fb * 9, ada_wk))
        for fb in range(2):
            jobs.append((kv_ada_w, fb * 1024, 1024, 144 + fb * 8, kv_ada_wk))
        for ji, (w, f0, fw, col0, wkey) in enumerate(jobs):
            b = ji % 2
            wv = w.rearrange("(k p) f -> p k f", p=128)
            K.dma("sp", wb[b][:, :, :fw], wv[:, :, f0:f0 + fw], reads=[wkey] if wkey else [], writes=[("modw", b)])
            nch = fw // 128
            pb = ji % 2
            for j in range(nch):
                for k in range(8):
                    K.op("pe", lambda e, b=b, j=j, k=k, pb=pb: e.matmul(
                        ps[pb][:, j:j + 1], lhsT=wb[b][:, k, j * 128:(j + 1) * 128], rhs=cact[:, k:k + 1],
                        start=(k == 0), stop=(k == 7)),
                        reads=[("modw", b), "cact"], writes=[("ps", pb)], sig=(k == 7 and j == nch - 1))
            K.op("dve", lambda e, pb=pb, col0=col0, nch=nch: e.tensor_tensor(
                out=modraw[:, col0:col0 + nch], in0=ps[pb][:, 0:nch], in1=bias_all[:, col0:col0 + nch], op=ALU.add),
                reads=[("ps", pb), "bias_all"], writes=["modraw"])
        for l in range(2):
            for s_ in range(3):
                ni = l * 4 + s_
                base = l * 72 + s_ * 24
                K.op("dve", lambda e, ni=ni, base=base: e.tensor_copy(out=modB[:, ni, :], in_=modraw[:, base:base + 8]),
                     reads=["modraw"], writes=["mod"])
                K.op("dve", lambda e, ni=ni, base=base, l=l, s_=s_: e.scalar_tensor_tensor(
                    out=modA[:, ni, :], in0=modraw[:, base + 8:base + 16], scalar=1.0,
                    in1=ng_all[:, (l * 3 + s_) * 8:(l * 3 + s_) * 8 + 8], op0=ALU.add, op1=ALU.mult),
                    reads=["modraw", "ng_all"], writes=["mod"])
                gsc = 1.0 if s_ == 1 else 0.5
                K.op("dve", lambda e, l=l, s_=s_, base=base, gsc=gsc: e.tensor_scalar(
                    out=modG[:, l * 3 + s_, :], in0=modraw[:, base + 16:base + 24], scalar1=gsc, scalar2=0.0, op0=ALU.mult, op1=ALU.add),
                    reads=["modraw"], writes=["mod"])
        K.op("dve", lambda e: e.tensor_copy(out=modB[:, 3, :], in_=modraw[:, 144:152]), reads=["modraw"], writes=["mod"])
        K.op("dve", lambda e: e.scalar_tensor_tensor(out=modA[:, 3, :], in0=modraw[:, 152:160], scalar=1.0,
                                                     in1=ng_all[:, 48:56], op0=ALU.add, op1=ALU.mult),
             reads=["modraw", "ng_all"], writes=["mod"])
        K.op("dve", lambda e: e.tensor_copy(out=modA[:, 7, :], in_=ng_all[:, 56:64]), reads=["ng_all"], writes=["mod"])
        K.barrier()

    def XK(name, b):
        return [(name, b, c) for c in range(8)]

    def norm_mod(xt, xkeys, T, ni, sq, rs, tmp, h, hname, hb, psb, add_shift=True):
        K.op("act", lambda e: e.activation(out=sq[:, :, :T], in_=xt[:, :, :T], func=AF.Square), reads=xkeys, writes=["sq"])
        for c in range(8):
            K.op("pe", lambda e, c=c: e.matmul(ps[psb][:, :T], lhsT=ones_bf[:], rhs=sq[:, c, :T],
                                                start=(c == 0), stop=(c == 7)),
                 reads=["sq", "ones"], writes=[("ps", psb)], sig=(c == 7))
        K.op("act", lambda e: e.activation(out=rs[:, :T], in_=ps[psb][:, :T], func=AF.Sqrt, bias=EPS, scale=1.0 / D),
             reads=[("ps", psb)], writes=["rs"])
        K.op("dve", lambda e: e.reciprocal(out=rs[:, :T], in_=rs[:, :T]), reads=["rs"], writes=["rs"])
        for c in range(8):
            if not add_shift:
                K.op("dve", lambda e, c=c: e.scalar_tensor_tensor(
                    out=xt[:, c, :T], in0=xt[:, c, :T], scalar=modA[:, ni, c:c + 1], in1=rs[:, :T],
                    op0=ALU.mult, op1=ALU.mult), reads=[xkeys[c], "rs", "mod"], writes=[xkeys[c]])
                continue
            K.op("dve", lambda e, c=c: e.scalar_tensor_tensor(
                out=tmp[:, c, :T], in0=xt[:, c, :T], scalar=modA[:, ni, c:c + 1], in1=rs[:, :T],
                op0=ALU.mult, op1=ALU.mult), reads=[xkeys[c], "rs", "mod"], writes=[("tmp", c)])
            if add_shift:
                K.op("act", lambda e, c=c: e.activation(out=h[:, c, :T], in_=tmp[:, c, :T], func=AF.Identity,
                                                        bias=modB[:, ni, c:c + 1], scale=1.0),
                     reads=[("tmp", c), "mod"], writes=[(hname, hb, c)])

    def load_w(dst, dkey, w, kchunks, per=4):
        w_ap, wkey = w
        wv = w_ap.rearrange("(k p) f -> p k f", p=128)
        for k0 in range(0, kchunks, per):
            k1 = min(kchunks, k0 + per)
            K.dma("pool", dst[:, k0:k1, :], wv[:, k0:k1, :], reads=[wkey] if wkey else [], writes=[dkey])

    def phase_ffn(l, s_, src, dst, pre=None, final=False):
        T = 256
        ntile = NT // T
        ni = l * 4 + (0 if s_ == 0 else 2)
        gi = l * 3 + (0 if s_ == 0 else 2)
        with ExitStack() as st:
            win = sb("win", [128, 8, 2 * FH], BF16, st)
            wout = sb("wout", [128, 22, D], BF16, st)
            xt = [sb("xt%d" % i, [128, 8, T], F32, st) for i in range(2)]
            sq = sb("sq", [128, 8, T], BF16, st)
            rs = sb("rs", [128, T], F32, st)
            tmp = sb("tmp", [128, 8, T], F32, st)
            h = [sb("h%d" % i, [128, 8, T], BF16, st) for i in range(2)]
            sg = [sb("sg%d" % i, [128, T], F32, st) for i in range(2)]
            act = sb("act", [128, 22, T], BF16, st)
            if pre is not None:
                wo = sb("wo", [128, 8, D], BF16, st)
                at1 = sb("at", [128, 8, T], BF16, st)
                at = [at1, at1]
                load_w(wo, "wo", pre[0], 8)
            load_w(win, "win", ffn_w_in[l, s_], 8, per=1)
            load_w(wout, "wout", ffn_w_out[l, s_], 22, per=6)
            srcv = src.rearrange("(c p) t -> p c t", p=128)
            dstv = dst.rearrange("(c p) t -> p c t", p=128)
            if pre is not None:
                atv = attnT.rearrange("(c p) t -> p c t", p=128)

            def load(i):
                b = i % 2
                K.dma("sp", xt[b][:], srcv[:, :, i * T:(i + 1) * T], writes=XK("xt", b))
                if pre is not None:
                    K.dma("sp", at[b][:], atv[:, :, i * T:(i + 1) * T], writes=[("at", 0)])

            def prep(i):
                b = i % 2
                if pre is not None:
                    for dc in range(8):
                        pb = 5 + dc % 2
                        for c in range(8):
                            K.op("pe", lambda e, b=b, dc=dc, c=c, pb=pb: e.matmul(
                                ps[pb][:, :T], lhsT=wo[:, c, dc * 128:(dc + 1) * 128], rhs=at[b][:, c, :],
                                start=(c == 0), stop=(c == 7)),
                                reads=["wo", ("at", 0)], writes=[("ps", pb)], sig=(c == 7))
                        K.op("dve", lambda e, b=b, dc=dc, pb=pb: e.scalar_tensor_tensor(
                            out=xt[b][:, dc, :], in0=ps[pb][:, :T], scalar=modG[:, pre[1], dc:dc + 1],
                            in1=xt[b][:, dc, :], op0=ALU.mult, op1=ALU.add),
                            reads=[("ps", pb), ("xt", b, dc), "mod"], writes=[("xt", b, dc)])
                norm_mod(xt[b], XK("xt", b), T, ni, sq, rs, tmp, h[b], "h", b, 0)

            def pass1(i):
                b = i % 2
                for fc in range(22):
                    pg, pu = 1 + fc % 2, 3 + fc % 2
                    for k in range(8):
                        K.op("pe", lambda e, fc=fc, k=k, pg=pg: e.matmul(
                            ps[pg][:, :T], lhsT=win[:, k, fc * 128:(fc + 1) * 128], rhs=h[b][:, k, :],
                            start=(k == 0), stop=(k == 7)),
                            reads=["win", ("h", b, k)], writes=[("ps", pg)], sig=(k == 7))
                    for k in range(8):
                        K.op("pe", lambda e, fc=fc, k=k, pu=pu: e.matmul(
                            ps[pu][:, :T], lhsT=win[:, k, FH + fc * 128:FH + (fc + 1) * 128], rhs=h[b][:, k, :],
                            start=(k == 0), stop=(k == 7)),
                            reads=["win", ("h", b, k)], writes=[("ps", pu)], sig=(k == 7))
                    K.op("act", lambda e, fc=fc, pg=pg: e.activation(out=sg[fc % 2][:], in_=ps[pg][:, :T], func=AF.Silu),
                         reads=[("ps", pg)], writes=[("sg", fc % 2)])
                    K.op("dve", lambda e, fc=fc, pu=pu: e.tensor_tensor(out=act[:, fc, :], in0=ps[pu][:, :T],
                                                                        in1=sg[fc % 2][:], op=ALU.mult),
                         reads=[("ps", pu), ("sg", fc % 2)], writes=[("act", fc)])

            def pass2(i):
                b = i % 2
                for dc in range(8):
                    pb = 5 + dc % 2
                    for fc in range(22):
                        K.op("pe", lambda e, dc=dc, fc=fc, pb=pb: e.matmul(
                            ps[pb][:, :T], lhsT=wout[:, fc, dc * 128:(dc + 1) * 128], rhs=act[:, fc, :],
                            start=(fc == 0), stop=(fc == 21)),
                            reads=["wout", ("act", fc)], writes=[("ps", pb)], sig=(fc == 21))
                    K.op("dve", lambda e, dc=dc, pb=pb: e.scalar_tensor_tensor(
                        out=xt[b][:, dc, :], in0=ps[pb][:, :T], scalar=modG[:, gi, dc:dc + 1],
                        in1=xt[b][:, dc, :], op0=ALU.mult, op1=ALU.add),
                        reads=[("ps", pb), ("xt", b, dc), "mod"], writes=[("xt", b, dc)])

            def post(i):
                b = i % 2
                if final:
                    norm_mod(xt[b], XK("xt", b), T, 7, sq, rs, tmp, None, None, None, 7, add_shift=False)
                    K.dma("sp", dstv[:, :, i * T:(i + 1) * T], xt[b][:], reads=XK("xt", b))
                else:
                    K.dma("sp", dstv[:, :, i * T:(i + 1) * T], xt[b][:], reads=XK("xt", b))

            load(0)
            prep(0)
            for i in range(ntile):
                if i + 1 < ntile:
                    load(i + 1)
                pass1(i)
                if i + 1 < ntile:
                    prep(i + 1)
                pass2(i)
                post(i)
            K.barrier()

    def phase_proj(src, ni, w_ap, nout, fm_outs, tm=None, fgate=False):
        T = 512
        ntile = NT // T
        with ExitStack() as st:
            w = sb("pw", [128, 8, nout], BF16, st)
            xt = [sb("pxt%d" % i, [128, 8, T], F32, st) for i in range(2)]
            sq = sb("psq", [128, 8, T], BF16, st)
            rs = sb("prs", [128, T], F32, st)
            tmp = sb("ptmp", [128, 8, T], F32, st)
            h = sb("ph", [128, 8, T], BF16, st)
            nfm = sum(n for _, n, _, _ in fm_outs)
            stage = sb("pstage", [128, max(nfm, 1), T], BF16, st)
            if tm is not None:
                hd = tm[1]
                if hd == 128:
                    vst = sb("pvst", [128, 4, D], BF16, st)
                else:
                    vstB = [sb("pvstB%d" % q_, [128, 16, 4, 65], BF16, st) for q_ in range(2)]
                    for q_ in range(2):
                        K.op("pool", lambda e, q_=q_: e.memset(vstB[q_][:], 1.0),
                             writes=[("vstB", q_, a_, b_) for a_ in range(4) for b_ in range(2)])
            if fgate:
                fgw = sb("fgw", [128, 8, 16], BF16, st)
                K.dma("pool", fgw[:], fgate_wT.rearrange("p (k h) -> p k h", h=16), writes=["fgw"])
                e1 = sb("fge1", [16, T], F32, st)
                lgs = sb("fglg", [16, T], F32, st)
            load_w(w, "pw", w_ap, 8, per=2)
            srcv = src.rearrange("(c p) t -> p c t", p=128)

            K.dma("sp", xt[0][:], srcv[:, :, 0:T], writes=XK("pxt", 0))
            for i in range(ntile):
                b = i % 2
                if i + 1 < ntile:
                    K.dma("sp", xt[1 - b][:], srcv[:, :, (i + 1) * T:(i + 2) * T], writes=XK("pxt", 1 - b))
                norm_mod(xt[b], XK("pxt", b), T, ni, sq, rs, tmp, h, "ph", 0, 0)
                hk = [("ph", 0, c) for c in range(8)]
                si = 0
                g = 0
                for (col0, nch, dstd, scale) in fm_outs:
                    for oc in range(nch):
                        pb = 1 + g % 4
                        for k in range(8):
                            K.op("pe", lambda e, k=k, pb=pb, c0=col0 + oc * 128: e.matmul(
                                ps[pb][:, :T], lhsT=w[:, k, c0:c0 + 128], rhs=h[:, k, :],
                                start=(k == 0), stop=(k == 7)),
                                reads=["pw", hk[k]], writes=[("ps", pb)], sig=(k == 7))
                        if g % 2 == 0:
                            K.op("act", lambda e, pb=pb, si=si, scale=scale: e.activation(
                                out=stage[:, si, :], in_=ps[pb][:, :T], func=AF.Copy, scale=scale),
                                reads=[("ps", pb)], writes=[("stage", si)])
                        else:
                            K.op("dve", lambda e, pb=pb, si=si, scale=scale: e.tensor_scalar(
                                out=stage[:, si, :], in0=ps[pb][:, :T], scalar1=scale, scalar2=0.0, op0=ALU.mult, op1=ALU.add),
                                reads=[("ps", pb)], writes=[("stage", si)])
                        si += 1
                        g += 1
                    dv = dstd.rearrange("(c p) t -> p c t", p=128)
                    K.dma("sp", dv[:, :, i * T:(i + 1) * T], stage[:, si - nch:si, :],
                          reads=[("stage", j) for j in range(si - nch, si)])
                if tm is not None:
                    col0, hd, dstd = tm
                    for tb in range(4):
                        for eh in range(2):
                            pb = 1 + g % 4
                            for k in range(8):
                                K.op("pe", lambda e, k=k, pb=pb, tb=tb, c0=col0 + eh * 512: e.matmul(
                                    ps[pb][:, :512], lhsT=h[:, k, tb * 128:(tb + 1) * 128], rhs=w[:, k, c0:c0 + 512],
                                    start=(k == 0), stop=(k == 7)),
                                    reads=["pw", hk[k]], writes=[("ps", pb)], sig=(k == 7))
                            if hd == 128:
                                o_ap = vst[:, tb, eh * 512:(eh + 1) * 512]
                                i_ap = ps[pb][:, :512]
                            else:
                                o_ap = vstB[i % 2][:, eh * 8:(eh + 1) * 8, tb, 0:64]
                                i_ap = ps[pb][:, :512].rearrange("p (h e) -> p h e", e=64)
                            vkey = ("vst", tb, eh) if hd == 128 else ("vstB", i % 2, tb, eh)
                            if g % 2 == 0:
                                K.op("act", lambda e, o_ap=o_ap, i_ap=i_ap: e.activation(out=o_ap, in_=i_ap, func=AF.Copy),
                                     reads=[("ps", pb)], writes=[vkey])
                            else:
                                K.op("dve", lambda e, o_ap=o_ap, i_ap=i_ap: e.tensor_copy(out=o_ap, in_=i_ap),
                                     reads=[("ps", pb)], writes=[vkey])
                            g += 1
                        blk = i * 4 + tb
                        if hd == 128:
                            dv = dstd.rearrange("(h p) (b e) -> p h b e", p=128, e=128)
                            K.dma("sp", dv[:, :, blk, :], vst[:, tb, :].rearrange("p (h e) -> p h e", e=128), reads=[("vst", tb, 0), ("vst", tb, 1)])
                        elif tb == 3:
                            dv = dstd.rearrange("(h p) c -> p h c", p=128)
                            K.dma("sp", dv[:, :, i * 260:(i + 1) * 260], vstB[i % 2][:].rearrange("p h t e -> p h (t e)"),
                                  reads=[("vstB", i % 2, a_, b_) for a_ in range(4) for b_ in range(2)])
                if fgate:
                    for k in range(8):
                        K.op("pe", lambda e, k=k: e.matmul(ps[5][0:16, :T], lhsT=fgw[:, k, :], rhs=h[:, k, :],
                                                            start=(k == 0), stop=(k == 7)),
                             reads=["fgw", hk[k]], writes=[("ps", 5)], sig=(k == 7))
                    K.op("act", lambda e: e.activation(out=e1[:], in_=ps[5][0:16, :T], func=AF.Exp, bias=nfgb[:, 0:1], scale=-1.0),
                         reads=[("ps", 5), "nfgb"], writes=["e1"])
                    K.op("act", lambda e: e.activation(out=e1[:], in_=e1[:], func=AF.Ln, bias=1.0, scale=1.0),
                         reads=["e1"], writes=["e1"])
                    K.op("dve", lambda e: e.tensor_scalar(out=lgs[:], in0=e1[:], scalar1=-1.0, scalar2=0.0, op0=ALU.mult, op1=ALU.add),
                         reads=["e1"], writes=["lgs"])
                    K.dma("sp", lf_loc[:, i * T:(i + 1) * T], lgs[:], reads=["lgs"])
            K.barrier()

    def setup_A(st):
        oh = sb("oh", [128, 2, 32, 128], F32, st)
        ngc = sb("ngc", [128, 128], F32, st)
        Tn = sb("Tn", [128, 32, 8], F32, st)
        DP = sb("DP", [128, 8, 2, 128], F32, st)
        cf = sb("cf", [128, NZ * 4, 3], F32, st)
        K.dma("sp", oh[:], onehot.rearrange("p (a b q) -> p a b q", a=2, b=32), writes=["oh"])
        K.dma("sp", ngc[:], negc, writes=["ngc"])
        K.dma("sp", Tn[:], relb.rearrange("p (b h) -> p b h", h=8), writes=["Tn"])
        K.dma("sp", cf[:], coefA.rearrange("p (z c) -> p z c", c=3), writes=["cf"])
        for bk in range(31):
            K.op("dve", lambda e, bk=bk: e.tensor_tensor(out=Tn[:, bk, :], in0=Tn[:, bk, :], in1=Tn[:, 31, :],
                                                          op=ALU.subtract), reads=["Tn"], writes=["Tn"])
        for hh in range(8):
            K.op("dve", lambda e, hh=hh: e.tensor_copy(out=DP[:, hh, 0, :], in_=ngc[:]), reads=["ngc"], writes=["DP"])
            K.op("dve", lambda e, hh=hh: e.memset(DP[:, hh, 1, :], 0.0), writes=["DP"])
            for a_ in range(2):
                for bk in range(31):
                    K.op("dve", lambda e, hh=hh, a_=a_, bk=bk: e.scalar_tensor_tensor(
                        out=DP[:, hh, a_, :], in0=oh[:, a_, bk, :], scalar=Tn[:, bk, hh:hh + 1],
                        in1=DP[:, hh, a_, :], op0=ALU.mult, op1=ALU.add),
                        reads=["oh", "Tn", "DP"], writes=["DP"])
        return oh, ngc, Tn, DP, cf

    def phase_attn(layer, preA=None):
        isA = layer == 0
        NH = 8 if isA else 16
        KC = 128 if isA else 70
        nmap = 2 if isA else 1
        VW = 128 if isA else 65
        TQ = 512
        ntile = NT // TQ
        with ExitStack() as st:
            Kt = [sb("Kt%d" % i, [KC, NR * NT], BF16, st) for i in range(2)]
            Qt = [sb("Qt%d" % i, [KC, NT], BF16, st) for i in range(2)]
            Vt = [sb("Vt%d" % i, [128, NR * NBLK, VW], BF16, st) for i in range(2)]
            rz = sb("rz", [128, TQ], F32, st)
            ast = [sb("ast%d" % i, [128, TQ], BF16, st) for i in range(2)]
            if isA:
                Mk = [sb("Mk%d" % i, [128, NZ, TQ], BF16, st) for i in range(2)]
                oh, ngc, Tn, DP, cf = preA
                t1 = sb("t1", [128, 128], F32, st)
                oA = sb("oA", [128, TQ], F32, st)
                on = sb("on", [128, TQ], F32, st)
                osq = sb("osq", [128, TQ], BF16, st)
                ors = sb("ors", [128, TQ], F32, st)
            else:
                Mk1 = sb("MkB", [128, NZ, TQ], BF16, st)
                Mk = [Mk1, Mk1]
                K.dma("pool", Mk1[:], maskB.rearrange("p (z q) -> p z q", q=TQ), writes=[("Mk", 0), ("Mk", 1)])
                bc = sb("bcB", [64, TQ], F32, st)
                for i in range(2):
                    K.op("pool", lambda e, i=i: e.memset(Kt[i][64:70, :], 1.0), writes=[("Kt", i)])
                    K.op("pool", lambda e, i=i: e.memset(Qt[i][64:70, :], 1.0), writes=[("Qt", i)])
                    K.op("pool", lambda e, i=i: e.memset(Vt[i][:, :, 64:65], 1.0), writes=[("Vt", i)])

            def load_head(hh):
                b = hh % 2
                if isA:
                    for r in range(NR):
                        for hf in range(2):
                            ro = ((hh * 2 + hf) * NR + r) * 64
                            K.dma("sp", Kt[b][hf * 64:(hf + 1) * 64, r * NT:(r + 1) * NT], kT_all[ro:ro + 64, :],
                                  writes=[("Kt", b)])
                            K.dma("sp", Vt[b][hf * 64:(hf + 1) * 64, r * NBLK:(r + 1) * NBLK, :],
                                  v_all[ro:ro + 64, :].rearrange("p (b e) -> p b e", e=128),
                                  writes=[("Vt", b)])
                    K.dma("sp", Qt[b][:], qT[hh * 128:(hh + 1) * 128, :], writes=[("Qt", b)])
                    for z in range(NZ):
                        for c in range(4):
                            zi = z * 4 + c
                            K.op("dve", lambda e, hh=hh, zi=zi: e.tensor_scalar(
                                out=t1[:], in0=DP[:, hh, 1, :], scalar1=cf[:, zi, 1:2], scalar2=cf[:, zi, 2:3],
                                op0=ALU.mult, op1=ALU.add), reads=["DP", "cf"], writes=["t1"])
                            K.op("dve", lambda e, hh=hh, zi=zi, z=z, c=c, b=b: e.scalar_tensor_tensor(
                                out=Mk[b][:, z, c * 128:(c + 1) * 128], in0=DP[:, hh, 0, :], scalar=cf[:, zi, 0:1],
                                in1=t1[:], op0=ALU.mult, op1=ALU.add), reads=["DP", "cf", "t1"], writes=[("Mk", b)])
                else:
                    for r in range(NR):
                        ro = (hh * NR + r) * 64
                        K.dma("sp", Kt[b][0:64, r * NT:(r + 1) * NT], kT_all[ro:ro + 64, :],
                              writes=[("Kt", b)])
                        K.dma("sp", Vt[b][:, r * NBLK:(r + 1) * NBLK, :],
                              vb_all[(hh * NR + r) * 128:(hh * NR + r + 1) * 128, :].rearrange("p (b e) -> p b e", e=65),
                              writes=[("Vt", b)])
                    K.dma("sp", Kt[b][67:70, :], FkT[hh * 3:(hh + 1) * 3, :], writes=[("Kt", b)])
                    K.dma("sp", Qt[b][0:64, :], qT[hh * 64:(hh + 1) * 64, :], writes=[("Qt", b)])
                    K.dma("sp", Qt[b][64:67, :], FqT[hh * 3:(hh + 1) * 3, :], writes=[("Qt", b)])

            it = [0]
            NPT = 4
            LOOKG = 2
            pt2 = [sb("pt2_%d" % i, [128, 2 * TQ], BF16, st) for i in range(NPT)]

            def attend(hh, j, m):
                b = hh % 2
                a_i = it[0] % 2
                it[0] += 1
                if isA:
                    po, pz = 6, 7
                else:
                    po, pz = 6 + a_i, None
                nz, zn = [], []
                for r in range(NR):
                    for i in range(4 * j + 4):
                        if i >= 4 * j:
                            zn.append((r * NBLK + i, r * 4 + (i - 4 * j)))
                        elif isA and r == NR - 1 and i == 4 * j - 1:
                            zn.append((r * NBLK + i, NZ - 1))
                        else:
                            nz.append((r * NBLK + i, None))
                groups = [nz[x:x + 2] for x in range(0, len(nz), 2)] + [zn[x:x + 2] for x in range(0, len(zn), 2)]
                ng = len(groups)
                nkb = len(nz) + len(zn)
                if isA:
                    k0, k1 = m * 64, m * 64 + 64
                else:
                    k0, k1 = 0, 70
                cnt = [0]
                for gi in range(ng + LOOKG):
                    if gi < ng:
                        grp = groups[gi]
                        s2 = gi % 3
                        W = TQ * len(grp)
                        for e_, (kb, z) in enumerate(grp):
                            bk = 2 * s2 + e_
                            zone = z is not None
                            K.op("pe", lambda e, kb=kb, bk=bk, zone=zone: e.matmul(
                                ps[bk][:, :TQ], lhsT=Kt[b][k0:k1, kb * 128:(kb + 1) * 128],
                                rhs=Qt[b][k0:k1, j * TQ:(j + 1) * TQ], start=True, stop=not zone),
                                reads=[("Kt", b), ("Qt", b)], writes=[("ps", bk)], sig=(e_ == len(grp) - 1) and not zone)
                            if zone:
                                K.op("pe", lambda e, bk=bk, z=z: e.matmul(
                                    ps[bk][:, :TQ], lhsT=ident_bf[:], rhs=Mk[b][:, z, :], start=False, stop=True),
                                    reads=["ident", ("Mk", b)], writes=[("ps", bk)], sig=(e_ == len(grp) - 1))
                        bks = [("ps", 2 * s2 + e_) for e_ in range(len(grp))]
                        if True:
                            K.op("act", lambda e, gi=gi, s2=s2, W=W: e.activation(
                                out=pt2[gi % NPT][:, :W], in_=psbig[:, 2 * s2 * TQ:2 * s2 * TQ + W], func=AF.Exp),
                                reads=bks, writes=[("pt2", gi % NPT)])
                    if gi >= LOOKG:
                        gp = gi - LOOKG
                        grp = groups[gp]
                        for e_, (kb, z) in enumerate(grp):
                            u = cnt[0]
                            cnt[0] += 1
                            K.op("pe", lambda e, kb=kb, u=u, e_=e_, gp=gp: e.matmul(
                                ps[po][0:VW, :TQ], lhsT=Vt[b][:, kb, :], rhs=pt2[gp % NPT][:, e_ * TQ:(e_ + 1) * TQ],
                                start=(u == 0), stop=(u == nkb - 1)),
                                reads=[("Vt", b), ("pt2", gp % NPT)], writes=[("ps", po)],
                                sig=(not isA) and e_ == len(grp) - 1)
                            if isA:
                                K.op("pe", lambda e, u=u, e_=e_, gp=gp: e.matmul(
                                    ps[pz][:, :TQ], lhsT=ones_bf[:], rhs=pt2[gp % NPT][:, e_ * TQ:(e_ + 1) * TQ],
                                    start=(u == 0), stop=(u == nkb - 1)),
                                    reads=["ones", ("pt2", gp % NPT)], writes=[("ps", pz)], sig=(e_ == len(grp) - 1))
                return po, pz

            def epilogue_A(hh, j, m, po, pz, scr):
                K.op("act", lambda e: e.activation(out=rz[:], in_=ps[pz][:, :TQ], func=AF.Ln), reads=[("ps", pz)], writes=["rz"])
                K.op("act", lambda e: e.activation(out=rz[:], in_=rz[:], func=AF.Exp, scale=-1.0), reads=["rz"], writes=["rz"])
                if m == 0:
                    K.op("dve", lambda e: e.tensor_tensor(out=oA[:], in0=ps[po][:, :TQ], in1=rz[:], op=ALU.mult),
                         reads=[("ps", po), "rz"], writes=["oA"])
                    return
                K.op("dve", lambda e: e.tensor_tensor(out=on[:], in0=ps[po][:, :TQ], in1=rz[:], op=ALU.mult),
                     reads=[("ps", po), "rz"], writes=["on"])
                K.op("dve", lambda e: e.scalar_tensor_tensor(out=on[:], in0=on[:], scalar=nlam[:, 0:1], in1=oA[:],
                                                             op0=ALU.mult, op1=ALU.add),
                     reads=["on", "oA", "nlam"], writes=["on"])
                K.op("act", lambda e: e.activation(out=osq[:], in_=on[:], func=AF.Square), reads=["on"], writes=["osq"])
                K.op("pe", lambda e: e.matmul(ps[scr][:, :TQ], lhsT=ones_bf[:], rhs=osq[:], start=True, stop=True),
                     reads=["ones", "osq"], writes=[("ps", scr)])
                K.op("act", lambda e: e.activation(out=ors[:], in_=ps[scr][:, :TQ], func=AF.Ln, bias=EPS, scale=1.0 / 128),
                     reads=[("ps", scr)], writes=["ors"])
                K.op("act", lambda e: e.activation(out=ors[:], in_=ors[:], func=AF.Exp, scale=-0.5), reads=["ors"], writes=["ors"])
                sb_ = j % 2
                K.op("dve", lambda e: e.scalar_tensor_tensor(out=ast[sb_][:], in0=on[:], scalar=gsub[:, 0:1], in1=ors[:],
                                                             op0=ALU.mult, op1=ALU.mult),
                     reads=["on", "ors", "gsub"], writes=[("ast", sb_)])
                K.dma("sp", attnT[hh * 128:(hh + 1) * 128, j * TQ:(j + 1) * TQ], ast[sb_][:], reads=[("ast", sb_)])

            def epilogue_B(hh, j, po):
                K.op("act", lambda e: e.activation(out=rz[64:65, :], in_=ps[po][64:65, :TQ], func=AF.Ln), reads=[("ps", po)], writes=["rz"])
                K.op("act", lambda e: e.activation(out=rz[64:65, :], in_=rz[64:65, :], func=AF.Exp, scale=-1.0), reads=["rz"], writes=["rz"])
                K.op("pe", lambda e: e.matmul(ps[0][0:64, :TQ], lhsT=ones_f[64:65, 0:64], rhs=rz[64:65, :], start=True, stop=True),
                     reads=["onesf", "rz"], writes=[("ps", 0)])
                K.op("act", lambda e: e.activation(out=bc[:], in_=ps[0][0:64, :TQ], func=AF.Copy), reads=[("ps", 0)], writes=["bc"])
                sb_ = j % 2
                K.op("dve", lambda e: e.tensor_tensor(out=ast[sb_][0:64, :], in0=ps[po][0:64, :TQ], in1=bc[:], op=ALU.mult),
                     reads=[("ps", po), "bc"], writes=[("ast", sb_)])
                K.dma("sp", attnT[hh * 64:(hh + 1) * 64, j * TQ:(j + 1) * TQ], ast[sb_][0:64, :], reads=[("ast", sb_)])

            load_head(0)
            for hh in range(NH):
                if hh + 1 < NH:
                    load_head(hh + 1)
                for j in range(ntile):
                    prev_pz = None
                    for m in range(nmap):
                        po, pz = attend(hh, j, m)
                        if isA:
                            epilogue_A(hh, j, m, po, pz, 0)
                            prev_pz = pz
                        else:
                            epilogue_B(hh, j, po)
            K.barrier()

    def phase_F():
        with ExitStack() as st:
            bufA = sb("fA", [16, S], F32, st)
            bufB = sb("fB", [16, S], F32, st)
            Fg = sb("fFg", [16, S], F32, st)
            FkS = sb("fFkS", [16, 3, NR * NT], BF16, st)
            Fq = sb("fFq", [16, NT], F32, st)
            FqS = sb("fFqS", [16, 3, NT], BF16, st)
            Lv = bufA[:].rearrange("h (i r p) -> h i r p", r=NR, p=128)
            for r in range(NR):
                K.dma("sp", Lv[:, :, r, :], lf_all[r * 16:(r + 1) * 16, :].rearrange("h (i p) -> h i p", p=128),
                      writes=["fA"])
            K.op("dve", lambda e: e.memset(bufB[:], 1.0), writes=["fB"])
            K.op("dve", lambda e: e.tensor_tensor_scan(out=Fg[:], data0=bufB[:], data1=bufA[:], initial=0.0,
                                                       op0=ALU.mult, op1=ALU.add),
                 reads=["fA", "fB"], writes=["Fg"])
            Fv = Fg[:].rearrange("h (i r p) -> h i r p", r=NR, p=128)

            def split3(src_ap, dst3, shp, r1, r1key, negate):
                sgn = -1.0 if negate else 1.0
                K.op("dve", lambda e: e.tensor_scalar(out=r1, in0=src_ap, scalar1=sgn, scalar2=0.0, op0=ALU.mult, op1=ALU.add),
                     reads=["Fg", "Fq"], writes=[r1key])
                for t in range(3):
                    K.op("dve", lambda e, t=t: e.tensor_copy(out=dst3(t), in_=r1), reads=[r1key], writes=["split"])
                    if t < 2:
                        K.op("dve", lambda e, t=t: e.tensor_tensor(out=r1, in0=r1, in1=dst3(t), op=ALU.subtract),
                             reads=[r1key, "split"], writes=[r1key])

            for r in range(NR):
                r1 = bufB[:, 0:NT].rearrange("h (i p) -> h i p", p=128)
                split3(Fv[:, :, r, :], lambda t, r=r: FkS[:, t, r * NT:(r + 1) * NT].rearrange("h (i p) -> h i p", p=128),
                       None, r1, "fB", True)
            if NR == 2:
                Fq3 = Fq[:].rearrange("h (i p) -> h i p", p=128)
                K.op("dve", lambda e: e.tensor_scalar(out=Fq3, in0=Fv[:, :, 0, :], scalar1=rsel_sb[0:16, 0:1], scalar2=0.0,
                                                      op0=ALU.mult, op1=ALU.add), reads=["Fg", "rsel"], writes=["Fq"])
                K.op("dve", lambda e: e.scalar_tensor_tensor(out=Fq3, in0=Fv[:, :, 1, :], scalar=rsel_sb[0:16, 1:2], in1=Fq3,
                                                             op0=ALU.mult, op1=ALU.add),
                     reads=["Fg", "rsel", "Fq"], writes=["Fq"])
                fq_src = Fq[:]
            else:
                fq_src = Fg[:]
            split3(fq_src, lambda t: FqS[:, t, :], None, bufA[:, 0:NT], "fA", False)
            K.dma("sp", FkT.rearrange("(h t) k -> h t k", t=3), FkS[:], reads=["split"])
            K.dma("sp", FqT.rearrange("(h t) k -> h t k", t=3), FqS[:], reads=["split"])
            K.barrier()

    def done(name):
        return stop_after == name

    def finish():
        K.barrier(cc=True)
        for dst_, src_ in dbg_copies:
            if any(src_ is w_ for w_ in coll_written):
                K.dma("sp", dst_, src_)
        K.barrier(cc=True)
        es.close()
        return nc

    phase_ffn(0, 0, xT_in, x_s)
    if done("ffn00"):
        return finish()
    phase_proj(x_s, 1, (a_w_qkv, a_w_qkvk), 3 * D, [(0, 8, qT, 0.125), (D, 8, kT_loc, 1.0)], tm=(2 * D, 128, v_loc))
    K.collective_rows(kT_loc, kT_all, 64)
    K.collective_rows(v_loc, v_all, 64)
    coll_written.extend([kT_loc, kT_all, v_loc, v_all])
    stA = ExitStack()
    preA = setup_A(stA)
    K.barrier(cc=True)
    if done("projA"):
        stA.close()
        return finish()
    phase_attn(0, preA)
    stA.close()
    if done("attnA"):
        return finish()
    phase_ffn(0, 1, x_s, x_s, pre=((a_w_o, a_w_ok), 1))
    if done("ffn01"):
        return finish()
    phase_proj(x_s, 3, (kv_w, kv_wk), 2 * D, [(0, 8, kT_loc, 1.0)], tm=(D, 64, vb_loc), fgate=True)
    K.collective_rows(kT_loc, kT_all, 64)
    K.collective_rows(vb_loc, vb_all, 128)
    K.collective_rows(lf_loc, lf_all, 16)
    coll_written.extend([vb_loc, vb_all, lf_loc, lf_all])
    K.barrier(cc=True)
    phase_F()
    if done("projKV"):
        return finish()
    phase_ffn(1, 0, x_s, x_s)
    phase_proj(x_s, 5, (b_w_q, b_w_qk), D, [(0, 8, qT, 0.125)])
    if done("projQ"):
        return finish()
    phase_attn(1)
    if done("attnB"):
        return finish()
    phase_ffn(1, 1, x_s, outT, pre=((b_w_o, b_w_ok), 4), final=True)
    return finish()


def _t5_bucket(n):
    n = np.maximum(n, 0)
    nf = np.maximum(n, 1).astype(np.float32)
    large = 16 + (np.log(nf / 16) / math.log(128 / 16) * 16).astype(np.int32)
    large = np.minimum(large, 31)
    return np.where(n < 16, n, large)


def _fm(v):
    return np.ascontiguousarray(v.reshape(-1, 128).T)


def make_in_maps(inp):
    f32 = np.float32
    x = np.asarray(inp["x"], f32)
    k = np.arange(128)[:, None]
    q = np.arange(128)[None, :]
    oh = np.zeros((128, 2, 32, 128), f32)
    relD = q - k
    bD = _t5_bucket(relD)
    relP = 128 + q - k
    bP = _t5_bucket(relP)
    for b in range(32):
        oh[:, 0, b, :] = ((bD == b) & (relD >= 0))
        oh[:, 1, b, :] = (bP == b)
    negc = np.where(relD < 0, NEG, 0.0).astype(f32)
    big = {
        "ada_w": np.asarray(inp["ada_w"], f32).reshape(2 * D, 9 * D),
        "kv_ada_w": np.asarray(inp["kv_ada_w"], f32),
        "a_w_qkv": np.asarray(inp["a_w_qkv"][0], f32),
        "a_w_o": np.asarray(inp["a_w_o"][0], f32),
        "kv_w": np.asarray(inp["kv_w"], f32),
        "b_w_q": np.asarray(inp["b_w_q"][0], f32),
        "b_w_o": np.asarray(inp["b_w_o"][0], f32),
    }
    for l_ in range(2):
        for s_ in range(2):
            big["ffn_w_in_%d%d" % (l_, s_)] = np.asarray(inp["ffn_w_in"][l_, s_], f32)
            big["ffn_w_out_%d%d" % (l_, s_)] = np.asarray(inp["ffn_w_out"][l_, s_], f32)
    shared = {
        "ada_bT": np.concatenate([_fm(np.asarray(inp["ada_b"][l], f32)) for l in range(2)], 1),
        "norm_gT": np.concatenate([_fm(np.asarray(inp["norm_g"][l, s], f32)) for l in range(2) for s in range(3)], 1),
        "a_lam": np.ascontiguousarray(np.asarray(inp["a_lambda"][0], f32).reshape(1, 256)),
        "a_subg": np.ascontiguousarray(np.asarray(inp["a_subln_g"][0], f32).reshape(128, 1)),
        "relb": np.ascontiguousarray(np.broadcast_to(np.asarray(inp["rel_bias"], f32).reshape(1, 256), (128, 256))),
        "kv_ada_bT": _fm(np.asarray(inp["kv_ada_b"], f32)),
        "kv_norm_gT": _fm(np.asarray(inp["kv_norm_g"], f32)),
        "fgate_wT": np.ascontiguousarray(np.asarray(inp["fgate_w"], f32).reshape(8, 128, 16).transpose(1, 0, 2).reshape(128, 128)),
        "fgate_bT": np.ascontiguousarray(np.asarray(inp["fgate_b"], f32).reshape(16, 1)),
        "final_gT": _fm(np.asarray(inp["final_g"], f32)),
        "onehot": oh.reshape(128, -1),
        "negc": negc,
        "ident": np.eye(128, dtype=f32),
    }
    per_rank = []
    for r in range(NR):
        coef = np.zeros((128, NZ, 4, 3), f32)
        mB = np.zeros((128, NZ, 4, 128), f32)
        for rp, a in [(rp_, a_) for rp_ in range(NR) for a_ in range(4)] + [(NR - 1, -1)]:
            if True:
                z = rp * 4 + a if a >= 0 else NZ - 1
                for c in range(4):
                    diff = NR * (a - c) + (rp - r)
                    if diff > 0:
                        coef[:, z, c, 2] = NEG
                        mB[:, z, c, :] = NEG
                    elif diff == 0:
                        coef[:, z, c, 0] = 1.0
                        mB[:, z, c, :] = negc
                    elif diff == -1:
                        coef[:, z, c, 1] = 1.0
        rs = np.zeros((128, 2), f32)
        rs[:, 0] = 1.0 - r
        rs[:, 1] = r
        per_rank.append({"coefA": coef.reshape(128, -1), "maskB": mB.reshape(128, -1), "rsel": rs})
    maps = []
    for core in range(NCORE):
        b, r = core // NR, core % NR
        xb = x[b].reshape(NBLK, NR, 128, D)[:, r].reshape(NT, D)
        m = dict(shared)
        m.update(per_rank[r])
        for wn, wa in big.items():
            if WSHARD:
                rows = wa.shape[0] // NCORE
                m[wn] = np.ascontiguousarray(wa[core * rows:(core + 1) * rows])
            else:
                m[wn] = np.ascontiguousarray(wa)
        m["xT"] = np.ascontiguousarray(xb.T)
        m["cT"] = _fm(np.asarray(inp["c"][b], f32))
        maps.append(m)
    return maps


def assemble(outs):
    y = np.zeros((NB, S, D), np.float32)
    for core in range(NCORE):
        b, r = core // NR, core % NR
        o = np.asarray(outs[core]).T.reshape(NBLK, 128, D)
        y[b].reshape(NBLK, NR, 128, D)[:, r] = o
    return y


def kernel(**inputs):
    nc = build()
    in_maps = make_in_maps(inputs)
    res = run_bass_kernel_spmd(nc, in_maps, core_ids=list(range(NCORE)))
    return assemble([res.results[i]["outT"] for i in range(NCORE)])
```
